# Optimizing a Trainium2 kernel written in Bass

```python
import jax, jax.numpy as jnp
from jax import lax
import numpy as np

D_MODEL = 1024
BATCH = 4
SEQ = 8192
DEPTH = 2

A_WIDTH = D_MODEL // 2
A_HEADS = 8
CONV_WIDTH = 3
B_WIDTH = D_MODEL // 2
POOL_WINDOWS = (2, 4, 8, 16)
B_GROUPS = len(POOL_WINDOWS)
B_GROUP_DIM = B_WIDTH // B_GROUPS
EVEN_IN = 4 * A_WIDTH + 2 * B_WIDTH
EVEN_MIX = A_WIDTH + B_WIDTH
C_WIDTH = D_MODEL
C_HEADS = 8
C_HEAD_DIM = C_WIDTH // C_HEADS
CHUNK = 128
ODD_IN = 3 * C_WIDTH
N_EVEN = (DEPTH + 1) // 2
N_ODD = DEPTH // 2
EPS = 1e-6

kernel_name = "hybrid_conv_pool_gmlp_trunk"


def rmsnorm(x, g):
    xf = x.astype(jnp.float32)
    y = xf * lax.rsqrt(jnp.mean(xf * xf, axis=-1, keepdims=True) + EPS)
    return (y * g.astype(jnp.float32)).astype(x.dtype)


def layernorm(x, g, b):
    xf = x.astype(jnp.float32)
    mu = jnp.mean(xf, axis=-1, keepdims=True)
    var = jnp.mean(jnp.square(xf - mu), axis=-1, keepdims=True)
    y = (xf - mu) * lax.rsqrt(var + EPS)
    return (y * g.astype(jnp.float32) + b.astype(jnp.float32)).astype(x.dtype)


def short_gated_conv(xa, gb, gc, conv_w):
    S = xa.shape[1]
    h = gc * xa
    hp = jnp.pad(h, ((0, 0), (CONV_WIDTH - 1, 0), (0, 0)))
    conv = sum(conv_w[k] * hp[:, k:k + S] for k in range(CONV_WIDTH))
    return gb * conv


def multiscale_pool(xp, pool_w, pool_scale):
    Bsz, S, _ = xp.shape
    xf = xp.astype(jnp.float32)
    cs = jnp.cumsum(xf, axis=1)
    pos = jnp.arange(S)
    outs = []
    for g, w in enumerate(POOL_WINDOWS):
        sl = slice(g * B_GROUP_DIM, (g + 1) * B_GROUP_DIM)
        cs_g = cs[..., sl]
        lower = jnp.pad(cs_g, ((0, 0), (w, 0), (0, 0)))[:, :S]
        count = jnp.minimum(pos + 1, w).astype(jnp.float32)[None, :, None]
        outs.append((cs_g - lower) / count - xf[..., sl])
    pooled = jnp.stack(outs, axis=2).astype(xp.dtype)
    mixed = jnp.einsum('bsgc,gcd->bsgd', pooled, pool_w)
    return mixed.reshape(Bsz, S, B_WIDTH) * pool_scale


def even_layer(h, w_in, conv_w, pool_w, pool_scale, w_out):
    proj = h @ w_in
    xa, gb, gc, za, xp, zp = jnp.split(
        proj, np.cumsum([A_WIDTH] * 4 + [B_WIDTH]).tolist(), axis=-1)
    ya = short_gated_conv(xa, gb, gc, conv_w) * jax.nn.silu(za)
    yb = multiscale_pool(xp, pool_w, pool_scale) * jax.nn.silu(zp)
    return jnp.concatenate([ya, yb], axis=-1) @ w_out


def odd_layer(h, w_in, ln_g, ln_b, w_s, b_s, w_out):
    Bsz, S, _ = h.shape
    proj = h @ w_in
    u, v, z = jnp.split(proj, 3, axis=-1)
    v = layernorm(v, ln_g, ln_b)
    vc = v.reshape(Bsz, S // CHUNK, CHUNK, C_HEADS, C_HEAD_DIM)
    ws = jnp.tril(w_s)
    sv = jnp.einsum('hts,bnshc->bnthc', ws, vc) + b_s.T[None, None, :, :, None]
    y = u * sv.reshape(Bsz, S, C_WIDTH) * jax.nn.silu(z)
    return y @ w_out


def setup_inputs(seed: int = 0) -> dict:
    key = jax.random.key(seed)
    ks = jax.random.split(key, 16)
    f32 = jnp.float32
    nrm = lambda k, shape, s: jax.random.normal(k, shape, f32) * s
    return {
        "x": nrm(ks[0], (BATCH, SEQ, D_MODEL), 1.0),
        "pre_norm": 1.0 + nrm(ks[1], (DEPTH, D_MODEL), 0.05),
        "post_norm": 1.0 + nrm(ks[2], (DEPTH, D_MODEL), 0.05),
        "even_w_in": nrm(ks[3], (N_EVEN, D_MODEL, EVEN_IN), D_MODEL ** -0.5),
        "even_conv_w": nrm(ks[4], (N_EVEN, CONV_WIDTH, A_WIDTH), CONV_WIDTH ** -0.5),
        "even_pool_w": nrm(ks[5], (N_EVEN, B_GROUPS, B_GROUP_DIM, B_GROUP_DIM), B_GROUP_DIM ** -0.5),
        "even_pool_scale": 1.0 + nrm(ks[6], (N_EVEN, B_WIDTH), 0.1),
        "even_w_out": nrm(ks[7], (N_EVEN, EVEN_MIX, D_MODEL), EVEN_MIX ** -0.5),
        "odd_w_in": nrm(ks[8], (N_ODD, D_MODEL, ODD_IN), D_MODEL ** -0.5),
        "odd_ln_g": 1.0 + nrm(ks[9], (N_ODD, C_WIDTH), 0.05),
        "odd_ln_b": nrm(ks[10], (N_ODD, C_WIDTH), 0.02),
        "odd_w_s": nrm(ks[11], (N_ODD, C_HEADS, CHUNK, CHUNK), CHUNK ** -0.5),
        "odd_b_s": 1.0 + nrm(ks[12], (N_ODD, C_HEADS, CHUNK), 0.1),
        "odd_w_out": nrm(ks[13], (N_ODD, C_WIDTH, D_MODEL), C_WIDTH ** -0.5),
    }


def reference(x, pre_norm, post_norm, even_w_in, even_conv_w, even_pool_w,
              even_pool_scale, even_w_out, odd_w_in, odd_ln_g, odd_ln_b,
              odd_w_s, odd_b_s, odd_w_out):
    for i in range(DEPTH):
        h = rmsnorm(x, pre_norm[i])
        j = i // 2
        if i % 2 == 0:
            m = even_layer(h, even_w_in[j], even_conv_w[j], even_pool_w[j],
                           even_pool_scale[j], even_w_out[j])
        else:
            m = odd_layer(h, odd_w_in[j], odd_ln_g[j], odd_ln_b[j],
                          odd_w_s[j], odd_b_s[j], odd_w_out[j])
        x = x + rmsnorm(m, post_norm[i])
    return x
```

```python
import numpy as np
import concourse.bass as bass
import concourse.mybir as mybir
from concourse.bass_utils import run_bass_kernel_spmd

F32 = mybir.dt.float32
BF16 = mybir.dt.bfloat16
AF = mybir.ActivationFunctionType
ALU = mybir.AluOpType

NCORES = 8
D = 1024
T = 256
NT = 16
HAL = 16
EPS = 1e-6
NA = 115
NA2 = 384
LAYERS = (0, 1)
OPT = dict(adds="pool", splitp1=0, wmerge=4, ntmp=6, wwin=8, xsplit=1, wfold=0, nsplit=1,
           prio="idx", ewin=64, eps=0.0, peswin=24, lat=0.3, dscale=1.0)


def _chunk_order(layer):
    if layer == 0:
        a = lambda j: [j, 8 + j, 4 + j, 12 + j]
        b = lambda j: [16 + j, 20 + j]
        return a(0) + a(1) + b(0) + b(1) + b(2) + b(3) + a(2) + a(3)
    o = list(range(8, 16))
    for h in range(8):
        o += [h, 16 + h]
    return o


def _permute_w_in(w, layer):
    cols = np.concatenate([np.arange(n * 128, (n + 1) * 128) for n in _chunk_order(layer)])
    return np.ascontiguousarray(w[:, cols])


class Buf:
    __slots__ = ("w", "r")

    def __init__(self):
        self.w = None
        self.r = []


class Sched:
    ENG = ("pe", "act", "dve", "pool", "sp")
    WINDOW = {"pe": OPT["peswin"], "act": OPT["ewin"], "dve": OPT["ewin"], "pool": OPT["ewin"], "sp": 1}
    LAT = OPT["lat"]

    def __init__(self):
        self.ops = []

    @staticmethod
    def _free(ap):
        n = 1
        for d in list(ap.shape)[1:]:
            n *= int(d)
        return n

    def _dur(self, eng, calls):
        t = 0.0
        for name, kw in calls:
            if eng == "pe":
                if name == "transpose":
                    t += 0.075
                else:
                    n = self._free(kw["rhs"])
                    f32 = kw["rhs"].dtype == F32
                    if n >= 512:
                        d = 0.243
                    elif n >= 256:
                        d = 0.115
                    else:
                        d = 0.075
                    t += d * (4.0 if f32 else 1.0)
            elif name == "dma_start":
                t += 0.15 if eng == "sp" else 0.8
            elif eng == "sp":
                t += 0.05
            else:
                ap = kw.get("out", kw.get("ap"))
                n = self._free(ap)
                if eng == "act":
                    t += 0.11 + n / 1150.0 + (0.1 if "accum_out" in kw else 0.0)
                    if not isinstance(kw.get("scale", 1.0), float):
                        t += 0.17
                    if not isinstance(kw.get("bias", 0.0), float):
                        t += 0.05
                elif eng == "dve":
                    if name == "scalar_tensor_tensor":
                        t += 0.08 + n / 640.0
                    elif name == "tensor_copy" and kw["out"].dtype == BF16 and kw["in_"].dtype == BF16:
                        t += 0.05 + n / 1600.0
                    elif name == "bn_stats":
                        t += 0.16 + n / 960.0
                    elif name == "bn_aggr":
                        t += 0.21
                    else:
                        t += 0.07 + n / 960.0
                else:
                    if name == "tensor_scalar":
                        t += 0.1 + n / 680.0
                    elif n <= 4:
                        t += 0.56
                    else:
                        t += 0.13 + n / 460.0
        if eng in ("act", "dve", "pool"):
            t *= OPT["dscale"]
        return t

    def op(self, eng, calls, reads=(), writes=(), sem=None, inc=1, mark=True, extra=()):
        if isinstance(calls, tuple):
            calls = [calls]
        deps = set(d for d in extra if d is not None)
        for b in reads:
            if b.w is not None:
                deps.add(b.w)
        for b in writes:
            if b.w is not None:
                deps.add(b.w)
            deps.update(b.r)
        i = len(self.ops)
        dma_bytes = 0
        for name, kw in calls:
            if name == "dma_start":
                src = kw["in_"]
                dma_bytes += self._free(src) * int(list(src.shape)[0]) * 4
        self.ops.append(dict(eng=eng, calls=calls, deps=deps, key=(sem if sem is not None else eng),
                             inc=inc, mark=mark, dur=self._dur(eng, calls), dma=dma_bytes))
        for b in reads:
            b.r.append(i)
        for b in writes:
            b.w = i
            b.r = []
        return i

    def schedule(self):
        ops = self.ops
        n = len(ops)
        queues = {e: [i for i in range(n) if ops[i]["eng"] == e] for e in self.ENG}
        pos = {e: 0 for e in self.ENG}
        done = [False] * n
        fin = [0.0] * n
        free = {e: 0.0 for e in self.ENG}
        order = {e: [] for e in self.ENG}
        remaining = n
        dma_free = 0.0
        succ = [[] for _ in range(n)]
        for i in range(n):
            for d in ops[i]["deps"]:
                succ[d].append(i)
        bl = [0.0] * n
        for i in range(n - 1, -1, -1):
            m = 0.0
            for j in succ[i]:
                if bl[j] > m:
                    m = bl[j]
            bl[i] = m + ops[i]["dur"]
        use_bl = OPT["prio"] == "bl"
        EPSW = OPT["eps"]
        while remaining:
            best = None
            for e in self.ENG:
                q = queues[e]
                p = pos[e]
                while p < len(q) and done[q[p]]:
                    p += 1
                pos[e] = p
                cnt = 0
                k = p
                w = self.WINDOW[e]
                while k < len(q) and cnt < w:
                    i = q[k]
                    k += 1
                    if done[i]:
                        continue
                    cnt += 1
                    o = ops[i]
                    st = free[e]
                    ok = True
                    for d in o["deps"]:
                        if not done[d]:
                            ok = False
                            break
                        t = fin[d] + (0.0 if ops[d]["eng"] == e else self.LAT)
                        if t > st:
                            st = t
                    if not ok:
                        continue
                    if use_bl:
                        stq = max(st, free[e])
                        key = (round(stq / EPSW) if EPSW > 0 else stq, -bl[i], i)
                        if best is None or key < best[3]:
                            best = (st, i, e, key)
                    else:
                        if best is None or st < best[0] - 1e-9 or (abs(st - best[0]) <= 1e-9 and i < best[1]):
                            best = (st, i, e, None)
                        if st <= free[e] + 1e-9:
                            break
            assert best is not None, "scheduler deadlock"
            st, i, e = best[0], best[1], best[2]
            done[i] = True
            free[e] = st + ops[i]["dur"]
            if ops[i]["dma"]:
                xs_ = max(free[e], dma_free)
                dma_free = xs_ + ops[i]["dma"] / 330e3
                fin[i] = dma_free + 1.5
            else:
                fin[i] = free[e]
            order[e].append(i)
            remaining -= 1
        self.order = order
        self.est_total = max(free.values())
        cnt = {}
        self.count = [0] * n
        for e in self.ENG:
            for i in order[e]:
                o = ops[i]
                if o["mark"]:
                    cnt[o["key"]] = cnt.get(o["key"], 0) + o["inc"]
                    self.count[i] = cnt[o["key"]]
        self.final_cnt = cnt

    def replay(self, dscale, lat):
        ops = self.ops
        base = OPT["dscale"]
        fin = {}
        free = {e: 0.0 for e in self.ENG}
        ptr = {e: 0 for e in self.ENG}
        dma_free = 0.0
        left = len(ops)
        pe_busy = 0.0
        while left:
            prog = False
            for e in self.ENG:
                q = self.order[e]
                while ptr[e] < len(q):
                    i = q[ptr[e]]
                    o = ops[i]
                    if any(d not in fin for d in o["deps"]):
                        break
                    st = free[e]
                    for d in o["deps"]:
                        t = fin[d] + (0.0 if ops[d]["eng"] == e else lat)
                        if t > st:
                            st = t
                    du = o["dur"] / base * dscale if e in ("act", "dve", "pool") and not o["dma"] else o["dur"]
                    free[e] = st + du
                    if e == "pe":
                        pe_busy += du
                    if o["dma"]:
                        xs_ = max(free[e], dma_free)
                        dma_free = xs_ + o["dma"] / 330e3
                        fin[i] = dma_free + 1.5
                    else:
                        fin[i] = free[e]
                    ptr[e] += 1
                    left -= 1
                    prog = True
            assert prog
        return max(free.values()), pe_busy

    def emit(self, eng, e, sems):
        ops = self.ops
        waited = {}
        for i in self.order[eng]:
            o = ops[i]
            need = {}
            for d in o["deps"]:
                od = ops[d]
                if od["eng"] == "pe" and eng == "pe":
                    continue
                k, v = od["key"], self.count[d]
                if v > waited.get(k, 0) and v > need.get(k, 0):
                    need[k] = v
            for k, v in need.items():
                waited[k] = v
                e.wait_ge(sems[k], v)
            ins = None
            for name, kw in o["calls"]:
                ins = getattr(e, name)(**kw)
            if o["mark"]:
                ins.then_inc(sems[o["key"]], o["inc"])


def build_nc(layers=LAYERS, ntiles=NT):
    TOK = ntiles * T
    nc = bass.Bass("TRN2", target_bir_lowering=False)
    dt = nc.dram_tensor
    x_d = dt("x", [TOK, D], F32, kind="ExternalInput").ap()
    xh_d = dt("xh", [HAL, D], F32, kind="ExternalInput").ap()
    cst_d = dt("cst", [128, NA], F32, kind="ExternalInput").ap()
    cst2_d = dt("cst2", [128, NA2], F32, kind="ExternalInput").ap()
    gpo_d = dt("gpo", [128, 2 * D], F32, kind="ExternalInput").ap()
    bsb_d = dt("bsb", [128, D], F32, kind="ExternalInput").ap()
    wst_d = dt("wst", [128, 8, 128], F32, kind="ExternalInput").ap()
    plw_d = dt("plw", [128, 4, 128], F32, kind="ExternalInput").ap()
    win_d = [dt("win0", [D, 3 * D], F32, kind="ExternalInput").ap(),
             dt("win1", [D, 3 * D], F32, kind="ExternalInput").ap()]
    wout_d = [dt("wout0", [D, D], F32, kind="ExternalInput").ap(),
              dt("wout1", [D, D], F32, kind="ExternalInput").ap()]
    out_d = dt("out", [TOK, D], F32, kind="ExternalOutput").ap()

    from contextlib import ExitStack
    with ExitStack() as es:
        def sb(name, shape, dtype):
            return es.enter_context(nc.sbuf_tensor(name, shape, dtype))

        def ps(name, shape, dtype):
            return es.enter_context(nc.psum_tensor(name, shape, dtype))

        Win = [sb(f"Win{l}", [128, 8, 3 * D], BF16) for l in range(2)]
        Wout = [sb(f"Wout{l}", [128, 8, D], BF16) for l in range(2)]
        xres = [sb(f"xres{i}", [128, 2, D], F32) for i in range(3)]
        xs = [sb(f"xs{i}", [128, 2, D], BF16) for i in range(2)]
        hT0 = sb("hT0", [128, 8, T], BF16)
        yT0 = sb("yT0", [128, 8, T], BF16)
        hbuf = [sb(f"hbuf{j}", [128, HAL + T], F32) for j in range(4)]
        xpbuf = [sb(f"xpbuf{j}", [128, HAL + T], F32) for j in range(4)]
        NTMP = OPT["ntmp"]
        tmp = [sb(f"tmp{i}", [128, HAL + T], F32) for i in range(NTMP)]
        pooled = sb("pooled", [128, T], BF16)
        hT1 = sb("hT1", [128, 8, T], BF16)
        yT1 = sb("yT1", [128, 8, T], BF16)
        cst = sb("cstA", [128, NA], F32)
        gpo = sb("gpo_sb", [128, 2, D], F32)
        cterm = sb("cterm", [128, 8, 128], F32)
        wsT = sb("wsT_bf", [128, 8, 128], BF16)
        plw = sb("plw_bf", [128, 4, 128], BF16)
        identb = sb("identb", [128, 128], BF16)
        small = sb("small", [128, 192], F32)

        hT = [hT0[:, :, :], hT1[:, :, :]]
        yT = [yT0[:, :, :], yT1[:, :, :]]
        hT.append(yT1[:, 3, 0:128].rearrange("p (k t) -> p k t", k=8))
        hT_off = [0, 0, T - HAL]

        psT = [ps(f"psT{i}", [128, D], BF16) for i in range(2)]
        psG = [ps(f"psG{i}", [128, 512], F32) for i in range(4)]
        psO = [ps(f"psO{i}", [128, 512], F32) for i in range(2)]

        sem_names = ["pe", "act", "dve", "pool", "ld0", "ld1", "ld2", "st0", "st1", "st2",
                     "su0", "su1", "su2", "su3", "su4", "su5", "su6"] + [f"w{l}{g}" for l in range(2) for g in range(4)]
        sems = {n: es.enter_context(nc.semaphore(n)) for n in sem_names}
        block = es.enter_context(nc.Block())

        S = Sched()

        gpreT = lambda l, dk: cst[:, l * 8 + dk:l * 8 + dk + 1]
        cwc = lambda j, k: cst[:, 16 + j * 3 + k:16 + j * 3 + k + 1]
        pscl = lambda j: cst[:, 28 + j:29 + j]
        lngc = lambda h: cst[:, 32 + h:33 + h]
        lnbc = lambda h: cst[:, 40 + h:41 + h]
        neghalf = cst[:, 112:113]
        neghalf2 = cst[:, 112:114]
        identf = tmp[NTMP - 1][:, 0:128]
        maskv = tmp[NTMP - 1][:, 128:256]
        onesf = tmp[NTMP - 2][:, 0:128]
        rcnt = lambda g: small[:, 128 + g * 16:128 + (g + 1) * 16]

        B_cst = Buf(); B_gpo = Buf(); B_cterm = Buf(); B_wsT = Buf(); B_plw = Buf()
        B_identb = Buf(); B_rcnt = Buf()
        B_x = [[Buf(), Buf()] for _ in range(3)]
        B_xsh = [[[Buf(), Buf()], [Buf(), Buf()]] for _ in range(2)]
        B_hT = [[Buf(), Buf()], [Buf(), Buf()]]
        B_yT = [[Buf() for _ in range(8)], [Buf() for _ in range(8)]]
        B_hT.append([B_yT[1][3], B_yT[1][3]])
        B_h = [Buf() for _ in range(4)]
        B_xp = [Buf() for _ in range(4)]
        B_tmp = [Buf() for _ in range(NTMP)]
        B_pooled = Buf()
        B_psT = [Buf(), Buf()]
        B_G = [Buf() for _ in range(4)]
        B_O = [Buf(), Buf()]
        B_small = {}

        small_init = [None]

        def smallbuf(name):
            if name not in B_small:
                B_small[name] = Buf()
                B_small[name].w = small_init[0]
            return B_small[name]

        tmp_rr = [0]

        def get_tmp():
            i = tmp_rr[0] % NTMP
            tmp_rr[0] += 1
            return tmp[i], B_tmp[i]


        wst_f = xres[2][:, 1, :].rearrange("p (h t) -> p h t", h=8)
        B_wstf = B_x[2][1]
        S.op("sp", ("dma_start", dict(out=cst[:, :], in_=cst_d[:, :])), writes=[B_cst], sem="su0", inc=16)
        B_c2 = B_tmp[NTMP - 1]
        B_c2o = B_tmp[NTMP - 2]
        S.op("sp", ("dma_start", dict(out=tmp[NTMP - 1][:, 0:256], in_=cst2_d[:, 0:256])), writes=[B_c2], sem="su1", inc=16)
        S.op("sp", ("dma_start", dict(out=tmp[NTMP - 2][:, 0:128], in_=cst2_d[:, 256:384])), writes=[B_c2o], sem="su6", inc=16)
        if 0 in layers:
            S.op("sp", ("dma_start", dict(out=xres[2][0:HAL, 0, :], in_=xh_d[:, :])),
                 writes=[B_x[2][0]], sem="ld2", inc=16)
        S.op("sp", ("dma_start", dict(out=xres[0][:, :, :],
                                      in_=x_d[0:T, :].rearrange("(tt p) d -> p tt d", p=128))),
             writes=[B_x[0][0], B_x[0][1]], sem="ld0", inc=16)
        if ntiles > 1:
            S.op("sp", ("dma_start", dict(out=xres[1][:, :, :],
                                          in_=x_d[T:2 * T, :].rearrange("(tt p) d -> p tt d", p=128))),
                 writes=[B_x[1][0], B_x[1][1]], sem="ld1", inc=16)
        S.op("pool", ("dma_start", dict(out=plw[:, :, :], in_=plw_d[:, :, :])), writes=[B_plw], sem="su4", inc=16)
        wtok = {l: [[], [], [], []] for l in range(2)}
        wready = {l: [None, None, None, None] for l in range(2)}

        wgroups = []
        cpos = {l: {n: p for p, n in enumerate(_chunk_order(l))} for l in range(2)}

        def wdma(l, grp, out_ap, in_ap):
            if not wgroups or wgroups[-1] is not wtok[l][grp]:
                wgroups.append(wtok[l][grp])
            gi = len(wgroups) - 1
            dep = list(wgroups[gi - 2]) if gi >= 2 else []
            t = S.op("pool", ("dma_start", dict(out=out_ap, in_=in_ap)), sem=f"w{l}{grp}", inc=16, extra=dep)
            wtok[l][grp].append(t)

        fold_rr = [0]

        def load_weights(l):
            g = OPT["wmerge"]
            for cb in range(3):
                for dk in range(0, 8, g):
                    wdma(l, cb, Win[l][:, dk:dk + g, cb * D:(cb + 1) * D],
                         win_d[l][dk * 128:(dk + g) * 128, cb * D:(cb + 1) * D].rearrange("(k p) n -> p k n", p=128))
                if OPT["wfold"]:
                    landed = list(wtok[l][cb])
                    toks = []
                    for dk in range(8):
                        eng = ("dve", "act", "pool", "dve", "act", "dve", "act", "pool")[fold_rr[0] % 8]
                        fold_rr[0] += 1
                        ap = Win[l][:, dk, cb * D:(cb + 1) * D]
                        if eng == "act":
                            call = ("activation", dict(out=ap, in_=ap, func=AF.Copy, scale=gpreT(l, dk)))
                        else:
                            call = ("tensor_scalar", dict(out=ap, in0=ap, scalar1=gpreT(l, dk), scalar2=0.0,
                                                          op0=ALU.mult, op1=ALU.add))
                        toks.append(S.op(eng, call, reads=[B_cst], extra=landed))
                    wready[l][cb] = toks
                else:
                    wready[l][cb] = wtok[l][cb]
            for ck in range(0, 8, g):
                wdma(l, 3, Wout[l][:, ck:ck + g, :],
                     wout_d[l][ck * 128:(ck + g) * 128, :].rearrange("(k p) n -> p k n", p=128))
            wready[l][3] = wtok[l][3]

        def emit_late_pieces(n):
            return

        late_pieces = []
        load_weights(layers[0])
        B_smallall = smallbuf("all")
        S.op("pool", ("memset", dict(ap=small[:, :], constant=0.0)), writes=[B_smallall])
        S.op("dve", ("reciprocal", dict(out=small[:, 128:192], in_=cst[:, 48:112])),
             reads=[B_cst], writes=[B_rcnt, B_smallall])
        small_init[0] = B_smallall.w
        for j in range(4):
            S.op("pool", ("memset", dict(ap=hbuf[j][:, :], constant=0.0)), writes=[B_h[j]])
            S.op("pool", ("memset", dict(ap=xpbuf[j][:, :], constant=0.0)), writes=[B_xp[j]])
        S.op("dve", ("tensor_copy", dict(out=identb[:, :], in_=identf)), reads=[B_c2], writes=[B_identb])
        if 1 in layers:
            S.op("sp", ("dma_start", dict(out=cterm[:, :, :], in_=bsb_d.rearrange("p (h t) -> p h t", h=8))),
                 writes=[B_cterm], sem="su2", inc=16)
            S.op("sp", ("dma_start", dict(out=wst_f, in_=wst_d[:, :, :])), writes=[B_wstf], sem="su3", inc=16)
        S.op("sp", ("dma_start", dict(out=gpo[:, :, :], in_=gpo_d.rearrange("p (l d) -> p l d", l=2))),
             writes=[B_gpo], sem="su5", inc=16)

        if 1 in layers:
            for h in range(8):
                S.op("dve", ("tensor_tensor", dict(out=wst_f[:, h, :], in0=wst_f[:, h, :], in1=maskv, op=ALU.mult)),
                     reads=[B_c2], writes=[B_wstf])
            S.op("dve", ("tensor_copy", dict(out=wsT[:, :, :], in_=wst_f)), reads=[B_wstf], writes=[B_wsT])
            for h in range(8):
                gi, hf = h // 4, h % 4
                S.op("pe", ("matmul", dict(out=psG[gi][:, hf * 128:(hf + 1) * 128], lhsT=onesf,
                                           rhs=wst_f[:, h, :], start=True, stop=True)),
                     reads=[B_c2o, B_wstf], writes=[B_G[gi]])
            for h in range(8):
                gi, hf = h // 4, h % 4
                S.op("dve", ("scalar_tensor_tensor", dict(
                    out=cterm[:, h, :], in0=psG[gi][:, hf * 128:(hf + 1) * 128], scalar=lnbc(h),
                    in1=cterm[:, h, :], op0=ALU.mult, op1=ALU.add)),
                    reads=[B_G[gi], B_cst], writes=[B_cterm])

        for l in layers[1:]:
            load_weights(l)

        ld_sem = ["ld0", "ld1", "ld2"]
        last_store = {}
        st_sem = ["st0", "st1", "st2"]

        def load_x(ti):
            b = ti % 3
            src = x_d[ti * T:(ti + 1) * T, :].rearrange("(tt p) d -> p tt d", p=128)
            S.op("sp", ("dma_start", dict(out=xres[b][:, :, :], in_=src)),
                 writes=[B_x[b][0], B_x[b][1]], sem=ld_sem[b], inc=16)

        def phase1(s, np_, c0, W, l, part_b_too=True, tts=None, xb=0, hs=None):
            nsub = (W + 127) // 128
            o = s * 56
            for tt in (range(nsub) if tts is None else tts):
                Bss, Bv1, Br = smallbuf(f"press{s}{tt}"), smallbuf(f"prev1{s}{tt}"), smallbuf(f"prer{s}{tt}")
                S.op("act", ("activation", dict(out=xs[s][0:np_, tt, :], in_=xres[xb][0:np_, tt, :],
                                                func=AF.Square, accum_out=small[0:np_, o + tt:o + tt + 1])),
                     reads=[B_x[xb][tt]], writes=[B_xsh[s][tt][0], B_xsh[s][tt][1], Bss])
                S.op("dve", ("tensor_scalar", dict(out=small[:, o + 2 + tt:o + 3 + tt], in0=small[:, o + tt:o + tt + 1],
                                                   scalar1=1.0 / D, scalar2=EPS, op0=ALU.mult, op1=ALU.add)),
                     reads=[Bss], writes=[Bv1])
                S.op("pool", ("tensor_tensor", dict(out=small[:, o + 4 + tt:o + 5 + tt], in0=small[:, o + 2 + tt:o + 3 + tt],
                                                    in1=neghalf, op=ALU.pow)),
                     reads=[Bv1, B_cst], writes=[Br])
                rs = small[0:np_, o + 4 + tt:o + 5 + tt]
                S.op("act", ("activation", dict(out=xs[s][0:np_, tt, 0:512], in_=xres[xb][0:np_, tt, 0:512], func=AF.Copy,
                                                scale=rs)),
                     reads=[B_x[xb][tt], Br], writes=[B_xsh[s][tt][0]])
                if OPT["xsplit"]:
                    S.op("dve", ("tensor_scalar", dict(out=xs[s][0:np_, tt, 512:1024], in0=xres[xb][0:np_, tt, 512:1024],
                                                       scalar1=rs, scalar2=None, op0=ALU.mult)),
                         reads=[B_x[xb][tt], Br], writes=[B_xsh[s][tt][1]])
                else:
                    S.op("act", ("activation", dict(out=xs[s][0:np_, tt, 512:1024], in_=xres[xb][0:np_, tt, 512:1024],
                                                    func=AF.Copy, scale=rs)),
                         reads=[B_x[xb][tt], Br], writes=[B_xsh[s][tt][1]])
            if not part_b_too:
                return
            phase1b(s, np_, c0, W, l, tts, hs)

        def phase1b(s, np_, c0, W, l, tts=None, hs=None):
            hs = s if hs is None else hs
            nsub = (W + 127) // 128
            for tt in (range(nsub) if tts is None else tts):
                for hf in range(2):
                    calls = [("transpose", dict(out=psT[tt][:, dk * 128:dk * 128 + np_],
                                                in_=xs[s][0:np_, tt, dk * 128:(dk + 1) * 128],
                                                identity=identb[0:np_, 0:np_])) for dk in range(4 * hf, 4 * hf + 4)]
                    S.op("pe", calls, reads=[B_xsh[s][tt][hf], B_identb], writes=[B_psT[tt]])
                cs0 = c0 + tt * 128 - hT_off[hs]
                if OPT["wfold"]:
                    S.op("dve", ("tensor_copy", dict(
                        out=hT[hs][:, :, cs0:cs0 + np_],
                        in_=psT[tt][:, :].rearrange("p (k t) -> p k t", k=8)[:, :, 0:np_])),
                        reads=[B_psT[tt]], writes=[B_hT[hs][tt]])
                else:
                    S.op("dve", ("tensor_tensor", dict(
                        out=hT[hs][:, :, cs0:cs0 + np_],
                        in0=psT[tt][:, :].rearrange("p (k t) -> p k t", k=8)[:, :, 0:np_],
                        in1=cst[:, l * 8:(l + 1) * 8].unsqueeze(2).to_broadcast([128, 8, np_]), op=ALU.mult)),
                        reads=[B_psT[tt], B_cst], writes=[B_hT[hs][tt]])

        def inproj_chunk(l, s, n, dst_ap, dstB, c0, W):
            p = cpos[l][n]
            calls = [("matmul", dict(out=dst_ap, lhsT=Win[l][:, dk, p * 128:(p + 1) * 128],
                                     rhs=hT[s][:, dk, c0 - hT_off[s]:c0 - hT_off[s] + W], start=(dk == 0), stop=(dk == 7)))
                     for dk in range(8)]
            S.op("pe", calls, reads=[B_hT[s][0], B_hT[s][1]], writes=[dstB], extra=wready[l][p // 8])

        gcount = [0]
        pending = []

        def even_group(l, s, ti, c0, W, kind, j):
            cs = slice(c0, c0 + W)
            hs = slice(HAL + c0, HAL + c0 + W)
            p = gcount[0] % 2
            gcount[0] += 1
            g0, g1 = psG[2 * p], psG[2 * p + 1]
            Bg0, Bg1 = B_G[2 * p], B_G[2 * p + 1]
            if kind == "A":
                xa_ps, gc_ps = g0[:, 0:W], g0[:, 256:256 + W]
                gb_ps, za_ps = g1[:, 0:W], g1[:, 256:256 + W]
                inproj_chunk(l, s, j, xa_ps, Bg0, c0, W)
                inproj_chunk(l, s, 8 + j, gc_ps, Bg0, c0, W)
                if ti >= 0:
                    inproj_chunk(l, s, 4 + j, gb_ps, Bg1, c0, W)
                    inproj_chunk(l, s, 12 + j, za_ps, Bg1, c0, W)
                for f in pending:
                    f()
                del pending[:]
                tA, BA = get_tmp()
                tS, BS = get_tmp()
                t1, B1 = get_tmp()
                S.op("act", ("activation", dict(out=tA[:, 0:W], in_=xa_ps, func=AF.Copy)), reads=[Bg0], writes=[BA])
                if ti >= 0:
                    S.op("act", ("activation", dict(out=tS[:, 0:W], in_=za_ps, func=AF.Silu)), reads=[Bg1], writes=[BS])
                if c0 == 0:
                    S.op("dve", ("tensor_copy", dict(out=hbuf[j][:, 0:HAL], in_=hbuf[j][:, T:T + HAL])),
                         reads=[B_h[j]], writes=[B_h[j]])
                S.op("dve", ("tensor_tensor", dict(out=hbuf[j][:, hs], in0=gc_ps, in1=tA[:, 0:W], op=ALU.mult)),
                     reads=[Bg0, BA], writes=[B_h[j]])
                if ti < 0:
                    return
                S.op("dve", ("tensor_tensor", dict(out=tS[:, 0:W], in0=gb_ps, in1=tS[:, 0:W], op=ALU.mult)),
                     reads=[Bg1], writes=[BS])
                S.op("pool", ("tensor_scalar", dict(out=t1[:, 0:W], in0=hbuf[j][:, HAL + c0 - 2:HAL + c0 - 2 + W],
                                                    scalar1=cwc(j, 0), scalar2=0.0, op0=ALU.mult, op1=ALU.add)),
                     reads=[B_h[j], B_cst], writes=[B1])
                S.op("dve", ("scalar_tensor_tensor", dict(
                    out=t1[:, 0:W], in0=hbuf[j][:, HAL + c0 - 1:HAL + c0 - 1 + W], scalar=cwc(j, 1),
                    in1=t1[:, 0:W], op0=ALU.mult, op1=ALU.add)), reads=[B_h[j], B_cst], writes=[B1])
                S.op("dve", ("scalar_tensor_tensor", dict(
                    out=t1[:, 0:W], in0=hbuf[j][:, hs], scalar=cwc(j, 2),
                    in1=t1[:, 0:W], op0=ALU.mult, op1=ALU.add)), reads=[B_h[j], B_cst], writes=[B1])
                S.op("pool", ("tensor_tensor", dict(out=yT[s][:, j, cs], in0=t1[:, 0:W], in1=tS[:, 0:W], op=ALU.mult)),
                     reads=[B1, BS], writes=[B_yT[s][j]])
            else:
                w = 2 << j
                xp_ps, zp_ps = g0[:, 0:W], g1[:, 0:W]
                inproj_chunk(l, s, 16 + j, xp_ps, Bg0, c0, W)
                if ti >= 0:
                    inproj_chunk(l, s, 20 + j, zp_ps, Bg1, c0, W)
                for f in pending:
                    f()
                del pending[:]
                tZ, BZ = get_tmp()
                if c0 == 0:
                    S.op("dve", ("tensor_copy", dict(out=xpbuf[j][:, 0:HAL], in_=xpbuf[j][:, T:T + HAL])),
                         reads=[B_xp[j]], writes=[B_xp[j]])
                S.op("act", ("activation", dict(out=xpbuf[j][:, hs], in_=xp_ps, func=AF.Copy)),
                     reads=[Bg0], writes=[B_xp[j]])
                if ti < 0:
                    return
                S.op("act", ("activation", dict(out=tZ[:, 0:W], in_=zp_ps, func=AF.Silu)), reads=[Bg1], writes=[BZ])
                src, Bsrc = xpbuf[j], B_xp[j]
                ws = 1
                while ws < w:
                    lo = HAL + c0 - (w - 2 * ws)
                    hi = HAL + c0 + W
                    dst, Bdst = get_tmp()
                    add_eng = "pool" if (OPT["adds"] == "pool" or (OPT["adds"] == "mixed" and 2 * ws < w)) else "dve"
                    S.op(add_eng, ("tensor_tensor", dict(out=dst[:, lo:hi], in0=src[:, lo:hi],
                                                        in1=src[:, lo - ws:hi - ws], op=ALU.add)),
                         reads=[Bsrc], writes=[Bdst])
                    src, Bsrc = dst, Bdst
                    ws *= 2
                S.op("dve", ("scalar_tensor_tensor", dict(
                    out=pooled[:, cs], in0=src[:, hs], scalar=1.0 / w, in1=xpbuf[j][:, hs],
                    op0=ALU.mult, op1=ALU.subtract)), reads=[Bsrc, B_xp[j]], writes=[B_pooled])
                if ti == 0 and c0 == 0:
                    t16, B16 = get_tmp()
                    S.op("dve", ("tensor_tensor", dict(out=t16[:, 0:HAL], in0=src[:, HAL:2 * HAL], in1=rcnt(j),
                                                       op=ALU.mult)), reads=[Bsrc, B_rcnt], writes=[B16])
                    S.op("dve", ("tensor_tensor", dict(out=pooled[:, 0:HAL], in0=t16[:, 0:HAL],
                                                       in1=xpbuf[j][:, HAL:2 * HAL], op=ALU.subtract)),
                         reads=[B16, B_xp[j]], writes=[B_pooled])
                mx_ps = g1[:, 256:256 + W]

                def do_pool():
                    S.op("pe", ("matmul", dict(out=mx_ps, lhsT=plw[:, j, :], rhs=pooled[:, cs], start=True, stop=True)),
                         reads=[B_pooled, B_plw], writes=[Bg1])
                    S.op("dve", ("scalar_tensor_tensor", dict(
                        out=yT[s][:, 4 + j, cs], in0=mx_ps, scalar=pscl(j), in1=tZ[:, 0:W],
                        op0=ALU.mult, op1=ALU.mult)), reads=[Bg1, BZ, B_cst], writes=[B_yT[s][4 + j]])
                pending.append(do_pool)

        def even_phase2(l, s, ti, c0, W, late=0, hooks=None):
            gcount[0] = 1
            g = 0
            for j in range(4):
                for kind in ("A", "B"):
                    even_group(l, s, ti, c0, W, kind, j)
                    emit_late_pieces(late)
                    if hooks and g in hooks:
                        hooks[g]()
                    g += 1

        def odd_head(l, s, c0, W, h):
            cs = slice(c0, c0 + W)
            p = gcount[0] % 2
            gcount[0] += 1
            g0, g1 = psG[2 * p], psG[2 * p + 1]
            Bg0, Bg1 = B_G[2 * p], B_G[2 * p + 1]
            u_ps, z_ps = g0[:, 0:W], g1[:, 0:W]
            inproj_chunk(l, s, h, u_ps, Bg0, c0, W)
            inproj_chunk(l, s, 16 + h, z_ps, Bg1, c0, W)
            for f in pending:
                f()
            del pending[:]
            calls = [("matmul", dict(out=g0[:, 256 + tt * 128:256 + (tt + 1) * 128],
                                     lhsT=xs[s][:, tt, h * 128:(h + 1) * 128], rhs=wsT[:, h, :],
                                     start=True, stop=True)) for tt in range(2)]
            S.op("pe", calls, reads=[B_xsh[s][0][h // 4], B_xsh[s][1][h // 4], B_wsT], writes=[Bg0])
            tZ, BZ = get_tmp()
            tV, BV = get_tmp()
            S.op("act", ("activation", dict(out=tZ[:, 0:W], in_=z_ps, func=AF.Silu)), reads=[Bg1], writes=[BZ])
            for tt in range(2):
                S.op("dve", ("scalar_tensor_tensor", dict(
                    out=tV[:, tt * 128:(tt + 1) * 128], in0=g0[:, 256 + tt * 128:256 + (tt + 1) * 128],
                    scalar=lngc(h), in1=cterm[:, h, :], op0=ALU.mult, op1=ALU.add)),
                    reads=[Bg0, B_cterm, B_cst], writes=[BV])
            S.op("dve", ("tensor_tensor", dict(out=tZ[:, 0:W], in0=u_ps, in1=tZ[:, 0:W], op=ALU.mult)),
                 reads=[Bg0], writes=[BZ])
            S.op("pool", ("tensor_tensor", dict(out=yT[s][:, h, cs], in0=tV[:, 0:W], in1=tZ[:, 0:W], op=ALU.mult)),
                 reads=[BV, BZ], writes=[B_yT[s][h]])

        def odd_phase2(l, s, ti, c0, W, hooks=None):
            o = s * 56
            for tt in range(2):
                banks = psO if tt == 0 else psG[0:2]
                Bb = B_O if tt == 0 else B_G[0:2]
                for eh in range(2):
                    if OPT["nsplit"]:
                        calls = [("matmul", dict(out=banks[eh][:, hh * 256:(hh + 1) * 256],
                                                 lhsT=hT[s][:, dk, tt * 128:(tt + 1) * 128],
                                                 rhs=Win[l][:, dk, eh * 512 + hh * 256:eh * 512 + (hh + 1) * 256],
                                                 start=(dk == 0), stop=(dk == 7))) for hh in range(2) for dk in range(8)]
                    else:
                        calls = [("matmul", dict(out=banks[eh][:, :], lhsT=hT[s][:, dk, tt * 128:(tt + 1) * 128],
                                                 rhs=Win[l][:, dk, eh * 512:(eh + 1) * 512],
                                                 start=(dk == 0), stop=(dk == 7))) for dk in range(8)]
                    S.op("pe", calls, reads=[B_hT[s][tt]], writes=[Bb[eh]], extra=wready[l][0])
                Bst, Bmv = smallbuf(f"lnst{s}{tt}"), smallbuf(f"lnmv{s}{tt}")
                q = o + 16 + tt * 20
                st_ap = small[:, q:q + 12]
                mv_ap = small[:, q + 12:q + 14]
                v1_ap, r_ap, nm_ap = small[:, q + 14:q + 15], small[:, q + 15:q + 16], small[:, q + 16:q + 17]
                for eh in range(2):
                    S.op("dve", ("bn_stats", dict(out=st_ap[:, eh * 6:(eh + 1) * 6], in_=banks[eh][:, :])),
                         reads=[Bb[eh]], writes=[Bst])
                S.op("dve", ("bn_aggr", dict(out=mv_ap, in_=st_ap)), reads=[Bst], writes=[Bmv])
                Bv1, Br, Bnm = smallbuf(f"lnv1{s}{tt}"), smallbuf(f"lnr{s}{tt}"), smallbuf(f"lnnm{s}{tt}")
                S.op("dve", ("tensor_scalar", dict(out=v1_ap, in0=mv_ap[:, 1:2], scalar1=1.0, scalar2=EPS,
                                                   op0=ALU.mult, op1=ALU.add)), reads=[Bmv], writes=[Bv1])
                S.op("pool", ("tensor_tensor", dict(out=r_ap, in0=v1_ap, in1=neghalf, op=ALU.pow)),
                     reads=[Bv1, B_cst], writes=[Br])
                S.op("dve", ("scalar_tensor_tensor", dict(out=nm_ap, in0=mv_ap[:, 0:1], scalar=-1.0, in1=r_ap,
                                                          op0=ALU.mult, op1=ALU.mult)),
                     reads=[Bmv, Br], writes=[Bnm])
                for eh in range(2):
                    S.op("act", ("activation", dict(out=xs[s][:, tt, eh * 512:(eh + 1) * 512], in_=banks[eh][:, :],
                                                    func=AF.Identity, bias=nm_ap, scale=r_ap)),
                         reads=[Bb[eh], Br, Bnm], writes=[B_xsh[s][tt][eh]])
            gcount[0] = 1
            for h in range(8):
                odd_head(l, s, c0, W, h)
                if hooks and h in hooks:
                    hooks[h]()

        def phase3(l, s, ti, tt, last, alt=False):
            o = s * 56
            xb = ti % 3
            banks, Bb = (psG[0:2], B_G[0:2]) if alt else (psO, B_O)
            for eh in range(2):
                if OPT["nsplit"]:
                    calls = [("matmul", dict(out=banks[eh][:, hh * 256:(hh + 1) * 256],
                                             lhsT=yT[s][:, ck, tt * 128:(tt + 1) * 128],
                                             rhs=Wout[l][:, ck, eh * 512 + hh * 256:eh * 512 + (hh + 1) * 256],
                                             start=(ck == 0), stop=(ck == 7))) for hh in range(2) for ck in range(8)]
                else:
                    calls = [("matmul", dict(out=banks[eh][:, :], lhsT=yT[s][:, ck, tt * 128:(tt + 1) * 128],
                                             rhs=Wout[l][:, ck, eh * 512:(eh + 1) * 512],
                                             start=(ck == 0), stop=(ck == 7))) for ck in range(8)]
                S.op("pe", calls, reads=B_yT[s], writes=[Bb[eh]], extra=wready[l][3])
            q = o + 8 + tt * 4
            Bss = smallbuf(f"postss{s}{tt}")
            for eh in range(2):
                tj, bj = get_tmp()
                S.op("act", ("activation", dict(out=tj[:, 0:256].bitcast(BF16), in_=banks[eh][:, :], func=AF.Square,
                                                accum_out=small[:, q + eh:q + eh + 1])),
                     reads=[Bb[eh]], writes=[bj, Bss])
            Bs1, Br = smallbuf(f"posts1{s}{tt}"), smallbuf(f"postr{s}{tt}")
            S.op("dve", ("tensor_tensor", dict(out=small[:, q + 2:q + 3], in0=small[:, q:q + 1],
                                               in1=small[:, q + 1:q + 2], op=ALU.add)), reads=[Bss], writes=[Bs1])
            S.op("dve", ("tensor_scalar", dict(out=small[:, q + 2:q + 3], in0=small[:, q + 2:q + 3], scalar1=1.0 / D,
                                               scalar2=EPS, op0=ALU.mult, op1=ALU.add)), reads=[Bs1], writes=[Bs1])
            S.op("pool", ("tensor_tensor", dict(out=small[:, q + 3:q + 4], in0=small[:, q + 2:q + 3], in1=neghalf,
                                                op=ALU.pow)), reads=[Bs1, B_cst], writes=[Br])
            for eh in range(2):
                for qq in range(2):
                    tm, bm = get_tmp()
                    e0 = eh * 512 + qq * 256
                    S.op("dve", ("scalar_tensor_tensor", dict(
                        out=tm[:, 0:256], in0=banks[eh][:, qq * 256:(qq + 1) * 256], scalar=small[:, q + 3:q + 4],
                        in1=gpo[:, l, e0:e0 + 256], op0=ALU.mult, op1=ALU.mult)),
                        reads=[Bb[eh], Br, B_gpo], writes=[bm])
                    S.op("pool", ("tensor_tensor", dict(out=xres[xb][:, tt, e0:e0 + 256],
                                                        in0=xres[xb][:, tt, e0:e0 + 256], in1=tm[:, 0:256],
                                                        op=ALU.add)), reads=[bm], writes=[B_x[xb][tt]])
            if last and tt == 1:
                dst = out_d[ti * T:(ti + 1) * T, :].rearrange("(tt p) d -> p tt d", p=128)
                last_store[xb] = S.op("sp", ("dma_start", dict(out=dst, in_=xres[xb][:, :, :])),
                                     reads=[B_x[xb][0], B_x[xb][1]], sem=st_sem[xb], inc=16)

        nl = len(layers)
        loaded = {0, 1}

        def ensure_loaded(ti):
            if ti < ntiles and ti not in loaded:
                loaded.add(ti)
                load_x(ti)

        handed = [False]

        def handoff(s):
            return

        def do_p1(blk, tts=None):
            ti, s, l, li = blk
            ensure_loaded(ti)
            phase1(s, 128, 0, T, l, True, tts, ti % 3)

        def do_p3(blk, tt, alt=False):
            ti, s, l, li = blk
            phase3(l, s, ti, tt, li == nl - 1, alt)
            if li == nl - 1 and tt == 1:
                ensure_loaded(ti + 3)

        groups = []
        t0_ = 0
        sizes = []
        rem = ntiles
        while rem > 0:
            g = 3 if (rem >= 3 and rem != 4) else 2 if rem >= 2 else 1
            sizes.append(g)
            rem -= g
        for g in sizes:
            groups.append(list(range(t0_, t0_ + g)))
            t0_ += g
        blocks = []
        for grp in groups:
            for li, l in enumerate(layers):
                for ti in grp:
                    blocks.append((ti, len(blocks) % 2, l, li))
        super_ok = len(blocks) >= 2 and blocks[0][2] == 0 and blocks[1][2] == 0 and blocks[1][1] == 1
        if 0 in layers:
            phase1(0, HAL, T - HAL, HAL, 0, True, None, 2, 2)
            if not super_ok:
                even_phase2(0, 2, -1, T - HAL, HAL)
                for f in pending:
                    f()
                del pending[:]
        ensure_loaded(2)
        do_p1(blocks[0])
        k_start = 0
        if super_ok:
            b0, b1 = blocks[0], blocks[1]
            do_p1(b1)
            gcount[0] = 1
            for kind, j in (("A", 0), ("A", 1), ("B", 0), ("B", 1), ("B", 2), ("B", 3), ("A", 2), ("A", 3)):
                even_group(0, 2, -1, T - HAL, HAL, kind, j)
                even_group(0, b0[1], b0[0], 0, T, kind, j)
                even_group(0, b1[1], b1[0], 0, T, kind, j)
            for f in pending:
                f()
            del pending[:]
            if len(blocks) > 2:
                nx = blocks[2]
                chained0 = nx[0] == b0[0]
                if not chained0:
                    do_p1(nx)
                do_p3(b0, 0)
                if chained0:
                    do_p1(nx, [0])
                do_p3(b0, 1)
                if chained0:
                    do_p1(nx, [1])
            else:
                do_p3(b0, 0)
                do_p3(b0, 1)
                do_p3(b1, 0)
                do_p3(b1, 1)
            k_start = 2
        for k, blk in enumerate(blocks):
            if k < k_start:
                continue
            ti, s, l, li = blk
            prev = blocks[k - 1] if k > 0 else None
            nxt = blocks[k + 1] if k + 1 < len(blocks) else None
            emb3 = prev is not None and prev[0] != ti
            emb1 = nxt is not None and nxt[0] != ti
            chained = emb1 and emb3 and nxt[0] == prev[0]

            def h0(prev=prev, nxt=nxt, emb3=emb3, emb1=emb1, chained=chained):
                if emb1 and not chained:
                    do_p1(nxt)
                if emb3:
                    do_p3(prev, 0)
                    if chained:
                        do_p1(nxt, [0])

            def h2(prev=prev, nxt=nxt, emb3=emb3, emb1=emb1, chained=chained):
                if emb3:
                    do_p3(prev, 1)
                    if chained:
                        do_p1(nxt, [1])

            hooks = {0: h0, 2: h2}
            if l == 0:
                even_phase2(l, s, ti, 0, T, 0, hooks)
            else:
                odd_phase2(l, s, ti, 0, T, hooks)
            if nxt is None or nxt[0] == ti:
                for f in pending:
                    f()
                del pending[:]
                do_p3(blk, 0)
                do_p3(blk, 1, True)
                if nxt is not None:
                    do_p1(nxt)

        S.op("sp", ("nop", dict()), extra=list(last_store.values()), mark=False)
        S.schedule()

        @block.tensor
        def _(e):
            S.emit("pe", e, sems)

        @block.scalar
        def _(e):
            S.emit("act", e, sems)

        @block.vector
        def _(e):
            S.emit("dve", e, sems)

        @block.gpsimd
        def _(e):
            S.emit("pool", e, sems)

        @block.sync
        def _(e):
            S.emit("sp", e, sems)

    return nc


def _host_consts(core, pre_norm, post_norm, conv_w, pool_scale, ln_g, ln_b):
    c = np.zeros((128, NA), np.float32)
    for l in range(2):
        c[:, l * 8:(l + 1) * 8] = pre_norm[l].reshape(8, 128).T
    c[:, 16:28] = conv_w.reshape(3, 4, 128).transpose(2, 1, 0).reshape(128, 12)
    c[:, 28:32] = pool_scale.reshape(4, 128).T
    c[:, 32:40] = ln_g.reshape(8, 128).T
    c[:, 40:48] = ln_b.reshape(8, 128).T
    pos = np.arange(16)
    for g, w in enumerate((2, 4, 8, 16)):
        cnt = np.minimum(pos + 1, w) if core % 2 == 0 else np.full(16, w)
        c[:, 48 + g * 16:48 + (g + 1) * 16] = cnt.astype(np.float32)[None, :]
    c[:, 112:114] = -0.5
    return c


def _host_consts2():
    c = np.zeros((128, NA2), np.float32)
    c[:, 0:128] = np.eye(128, dtype=np.float32)
    s = np.arange(128)
    c[:, 128:256] = (s[:, None] <= s[None, :]).astype(np.float32)
    c[:, 256:384] = 1.0
    return c


_NC_CACHE = {}


def _run(layers, xf, pre_norm, post_norm, even_w_in, even_conv_w, even_pool_w, even_pool_scale,
         even_w_out, odd_w_in, odd_ln_g, odd_ln_b, odd_w_s, odd_b_s, odd_w_out):
    key = tuple(layers)
    if key not in _NC_CACHE:
        _NC_CACHE[key] = build_nc(layers)
    nc = _NC_CACHE[key]
    TOK = NT * T
    gpo = np.ascontiguousarray(np.broadcast_to(post_norm.reshape(1, 2 * D), (128, 2 * D)))
    bsb = np.ascontiguousarray(np.broadcast_to(odd_b_s[0].reshape(1, D), (128, D)))
    wst = np.ascontiguousarray(odd_w_s[0].transpose(2, 0, 1))
    plw = np.ascontiguousarray(even_pool_w[0].transpose(1, 0, 2))
    win0p = _permute_w_in(even_w_in[0], 0)
    win1p = _permute_w_in(odd_w_in[0], 1)
    in_maps = []
    for c in range(NCORES):
        lo = c * TOK
        if c % 2 == 0:
            xh = np.zeros((HAL, D), np.float32)
        else:
            xh = np.ascontiguousarray(xf[lo - HAL:lo])
        in_maps.append({
            "x": np.ascontiguousarray(xf[lo:lo + TOK]),
            "xh": xh,
            "cst": _host_consts(c, pre_norm, post_norm, even_conv_w[0], even_pool_scale[0],
                                odd_ln_g[0], odd_ln_b[0]),
            "cst2": _host_consts2(),
            "gpo": gpo, "bsb": bsb, "wst": wst, "plw": plw,
            "win0": win0p, "win1": win1p,
            "wout0": even_w_out[0], "wout1": odd_w_out[0],
        })
    res = run_bass_kernel_spmd(nc, in_maps, core_ids=list(range(NCORES)))
    return np.concatenate([np.asarray(r["out"]) for r in res.results], axis=0)


def kernel(**inputs):
    a = {k: np.ascontiguousarray(np.asarray(v, dtype=np.float32)) for k, v in inputs.items()}
    xf = a["x"].reshape(-1, D)
    out = _run(LAYERS, xf, a["pre_norm"], a["post_norm"], a["even_w_in"], a["even_conv_w"],
               a["even_pool_w"], a["even_pool_scale"], a["even_w_out"], a["odd_w_in"],
               a["odd_ln_g"], a["odd_ln_b"], a["odd_w_s"], a["odd_b_s"], a["odd_w_out"])
    return out.reshape(a["x"].shape).astype(np.float32)
```

```python
import numpy as np
import concourse.bass as bass
import concourse.mybir as mybir
from concourse.bass_utils import run_bass_kernel_spmd

F32 = mybir.dt.float32
BF16 = mybir.dt.bfloat16
AF = mybir.ActivationFunctionType
ALU = mybir.AluOpType

NCORES = 8
D = 1024
T = 256
NT = 16
HAL = 16
EPS = 1e-6
NA = 115
NA2 = 384
LAYERS = (0, 1)
OPT = dict(adds="mixed", splitp1=0, wmerge=4, ntmp=6, wwin=8, xsplit=1, wfold=0, nsplit=1,
           prio="idx", ewin=64, eps=0.0, peswin=24, lat=0.3, dscale=1.0)


def _chunk_order(layer):
    if layer == 0:
        a = lambda j: [j, 8 + j, 4 + j, 12 + j]
        b = lambda j: [16 + j, 20 + j]
        return a(0) + a(1) + b(0) + b(1) + b(2) + b(3) + a(2) + a(3)
    o = list(range(8, 16))
    for h in range(8):
        o += [h, 16 + h]
    return o


def _permute_w_in(w, layer):
    cols = np.concatenate([np.arange(n * 128, (n + 1) * 128) for n in _chunk_order(layer)])
    return np.ascontiguousarray(w[:, cols])


class Buf:
    __slots__ = ("w", "r")

    def __init__(self):
        self.w = None
        self.r = []


class Sched:
    ENG = ("pe", "act", "dve", "pool", "sp")
    WINDOW = {"pe": OPT["peswin"], "act": OPT["ewin"], "dve": OPT["ewin"], "pool": OPT["ewin"], "sp": 1}
    LAT = OPT["lat"]

    def __init__(self):
        self.ops = []

    @staticmethod
    def _free(ap):
        n = 1
        for d in list(ap.shape)[1:]:
            n *= int(d)
        return n

    def _dur(self, eng, calls):
        t = 0.0
        for name, kw in calls:
            if eng == "pe":
                if name == "transpose":
                    t += 0.075
                else:
                    n = self._free(kw["rhs"])
                    f32 = kw["rhs"].dtype == F32
                    if n >= 512:
                        d = 0.243
                    elif n >= 256:
                        d = 0.115
                    else:
                        d = 0.075
                    t += d * (4.0 if f32 else 1.0)
            elif name == "dma_start":
                t += 0.15 if eng == "sp" else 0.8
            elif eng == "sp":
                t += 0.05
            else:
                ap = kw.get("out", kw.get("ap"))
                n = self._free(ap)
                if eng == "act":
                    t += 0.11 + n / 1150.0 + (0.1 if "accum_out" in kw else 0.0)
                    if not isinstance(kw.get("scale", 1.0), float):
                        t += 0.17
                    if not isinstance(kw.get("bias", 0.0), float):
                        t += 0.05
                elif eng == "dve":
                    if name == "scalar_tensor_tensor":
                        t += 0.08 + n / 640.0
                    elif name == "tensor_copy" and kw["out"].dtype == BF16 and kw["in_"].dtype == BF16:
                        t += 0.05 + n / 1600.0
                    elif name == "bn_stats":
                        t += 0.16 + n / 960.0
                    elif name == "bn_aggr":
                        t += 0.21
                    else:
                        t += 0.07 + n / 960.0
                else:
                    if name == "tensor_scalar":
                        t += 0.1 + n / 680.0
                    elif n <= 4:
                        t += 0.56
                    else:
                        t += 0.13 + n / 460.0
        if eng in ("act", "dve", "pool"):
            t *= OPT["dscale"]
        return t

    def op(self, eng, calls, reads=(), writes=(), sem=None, inc=1, mark=True, extra=()):
        if isinstance(calls, tuple):
            calls = [calls]
        deps = set(d for d in extra if d is not None)
        for b in reads:
            if b.w is not None:
                deps.add(b.w)
        for b in writes:
            if b.w is not None:
                deps.add(b.w)
            deps.update(b.r)
        i = len(self.ops)
        dma_bytes = 0
        for name, kw in calls:
            if name == "dma_start":
                src = kw["in_"]
                dma_bytes += self._free(src) * int(list(src.shape)[0]) * 4
        self.ops.append(dict(eng=eng, calls=calls, deps=deps, key=(sem if sem is not None else eng),
                             inc=inc, mark=mark, dur=self._dur(eng, calls), dma=dma_bytes))
        for b in reads:
            b.r.append(i)
        for b in writes:
            b.w = i
            b.r = []
        return i

    def schedule(self):
        ops = self.ops
        n = len(ops)
        queues = {e: [i for i in range(n) if ops[i]["eng"] == e] for e in self.ENG}
        pos = {e: 0 for e in self.ENG}
        done = [False] * n
        fin = [0.0] * n
        free = {e: 0.0 for e in self.ENG}
        order = {e: [] for e in self.ENG}
        remaining = n
        dma_free = 0.0
        succ = [[] for _ in range(n)]
        for i in range(n):
            for d in ops[i]["deps"]:
                succ[d].append(i)
        bl = [0.0] * n
        for i in range(n - 1, -1, -1):
            m = 0.0
            for j in succ[i]:
                if bl[j] > m:
                    m = bl[j]
            bl[i] = m + ops[i]["dur"]
        use_bl = OPT["prio"] == "bl"
        EPSW = OPT["eps"]
        while remaining:
            best = None
            for e in self.ENG:
                q = queues[e]
                p = pos[e]
                while p < len(q) and done[q[p]]:
                    p += 1
                pos[e] = p
                cnt = 0
                k = p
                w = self.WINDOW[e]
                while k < len(q) and cnt < w:
                    i = q[k]
                    k += 1
                    if done[i]:
                        continue
                    cnt += 1
                    o = ops[i]
                    st = free[e]
                    ok = True
                    for d in o["deps"]:
                        if not done[d]:
                            ok = False
                            break
                        t = fin[d] + (0.0 if ops[d]["eng"] == e else self.LAT)
                        if t > st:
                            st = t
                    if not ok:
                        continue
                    if use_bl:
                        stq = max(st, free[e])
                        key = (round(stq / EPSW) if EPSW > 0 else stq, -bl[i], i)
                        if best is None or key < best[3]:
                            best = (st, i, e, key)
                    else:
                        if best is None or st < best[0] - 1e-9 or (abs(st - best[0]) <= 1e-9 and i < best[1]):
                            best = (st, i, e, None)
                        if st <= free[e] + 1e-9:
                            break
            assert best is not None, "scheduler deadlock"
            st, i, e = best[0], best[1], best[2]
            done[i] = True
            free[e] = st + ops[i]["dur"]
            if ops[i]["dma"]:
                xs_ = max(free[e], dma_free)
                dma_free = xs_ + ops[i]["dma"] / 330e3
                fin[i] = dma_free + 1.5
            else:
                fin[i] = free[e]
            order[e].append(i)
            remaining -= 1
        self.order = order
        self.est_total = max(free.values())
        cnt = {}
        self.count = [0] * n
        for e in self.ENG:
            for i in order[e]:
                o = ops[i]
                if o["mark"]:
                    cnt[o["key"]] = cnt.get(o["key"], 0) + o["inc"]
                    self.count[i] = cnt[o["key"]]
        self.final_cnt = cnt

    def replay(self, dscale, lat):
        ops = self.ops
        base = OPT["dscale"]
        fin = {}
        free = {e: 0.0 for e in self.ENG}
        ptr = {e: 0 for e in self.ENG}
        dma_free = 0.0
        left = len(ops)
        pe_busy = 0.0
        while left:
            prog = False
            for e in self.ENG:
                q = self.order[e]
                while ptr[e] < len(q):
                    i = q[ptr[e]]
                    o = ops[i]
                    if any(d not in fin for d in o["deps"]):
                        break
                    st = free[e]
                    for d in o["deps"]:
                        t = fin[d] + (0.0 if ops[d]["eng"] == e else lat)
                        if t > st:
                            st = t
                    du = o["dur"] / base * dscale if e in ("act", "dve", "pool") and not o["dma"] else o["dur"]
                    free[e] = st + du
                    if e == "pe":
                        pe_busy += du
                    if o["dma"]:
                        xs_ = max(free[e], dma_free)
                        dma_free = xs_ + o["dma"] / 330e3
                        fin[i] = dma_free + 1.5
                    else:
                        fin[i] = free[e]
                    ptr[e] += 1
                    left -= 1
                    prog = True
            assert prog
        return max(free.values()), pe_busy

    def emit(self, eng, e, sems):
        ops = self.ops
        waited = {}
        for i in self.order[eng]:
            o = ops[i]
            need = {}
            for d in o["deps"]:
                od = ops[d]
                if od["eng"] == "pe" and eng == "pe":
                    continue
                k, v = od["key"], self.count[d]
                if v > waited.get(k, 0) and v > need.get(k, 0):
                    need[k] = v
            for k, v in need.items():
                waited[k] = v
                e.wait_ge(sems[k], v)
            ins = None
            for name, kw in o["calls"]:
                ins = getattr(e, name)(**kw)
            if o["mark"]:
                ins.then_inc(sems[o["key"]], o["inc"])


def build_nc(layers=LAYERS, ntiles=NT):
    TOK = ntiles * T
    nc = bass.Bass("TRN2", target_bir_lowering=False)
    dt = nc.dram_tensor
    x_d = dt("x", [TOK, D], F32, kind="ExternalInput").ap()
    xh_d = dt("xh", [HAL, D], F32, kind="ExternalInput").ap()
    cst_d = dt("cst", [128, NA], F32, kind="ExternalInput").ap()
    cst2_d = dt("cst2", [128, NA2], F32, kind="ExternalInput").ap()
    gpo_d = dt("gpo", [128, 2 * D], F32, kind="ExternalInput").ap()
    bsb_d = dt("bsb", [128, D], F32, kind="ExternalInput").ap()
    wst_d = dt("wst", [128, 8, 128], F32, kind="ExternalInput").ap()
    plw_d = dt("plw", [128, 4, 128], F32, kind="ExternalInput").ap()
    win_d = [dt("win0", [D, 3 * D], F32, kind="ExternalInput").ap(),
             dt("win1", [D, 3 * D], F32, kind="ExternalInput").ap()]
    wout_d = [dt("wout0", [D, D], F32, kind="ExternalInput").ap(),
              dt("wout1", [D, D], F32, kind="ExternalInput").ap()]
    out_d = dt("out", [TOK, D], F32, kind="ExternalOutput").ap()

    from contextlib import ExitStack
    with ExitStack() as es:
        def sb(name, shape, dtype):
            return es.enter_context(nc.sbuf_tensor(name, shape, dtype))

        def ps(name, shape, dtype):
            return es.enter_context(nc.psum_tensor(name, shape, dtype))

        Win = [sb(f"Win{l}", [128, 8, 3 * D], BF16) for l in range(2)]
        Wout = [sb(f"Wout{l}", [128, 8, D], BF16) for l in range(2)]
        xres = [sb(f"xres{i}", [128, 2, D], F32) for i in range(3)]
        xs = [sb(f"xs{i}", [128, 2, D], BF16) for i in range(2)]
        hT0 = sb("hT0", [128, 8, T], BF16)
        yT0 = sb("yT0", [128, 8, T], BF16)
        hbuf = [sb(f"hbuf{j}", [128, HAL + T], F32) for j in range(4)]
        xpbuf = [sb(f"xpbuf{j}", [128, HAL + T], F32) for j in range(4)]
        NTMP = OPT["ntmp"]
        tmp = [sb(f"tmp{i}", [128, HAL + T], F32) for i in range(NTMP)]
        pooled = sb("pooled", [128, T], BF16)
        hT1 = sb("hT1", [128, 8, T], BF16)
        yT1 = sb("yT1", [128, 8, T], BF16)
        cst = sb("cstA", [128, NA], F32)
        gpo = sb("gpo_sb", [128, 2, D], F32)
        cterm = sb("cterm", [128, 8, 128], F32)
        wsT = sb("wsT_bf", [128, 8, 128], BF16)
        plw = sb("plw_bf", [128, 4, 128], BF16)
        identb = sb("identb", [128, 128], BF16)
        small = sb("small", [128, 192], F32)

        hT = [hT0[:, :, :], hT1[:, :, :]]
        yT = [yT0[:, :, :], yT1[:, :, :]]
        hT.append(yT1[:, 3, 0:128].rearrange("p (k t) -> p k t", k=8))
        hT_off = [0, 0, T - HAL]

        psT = [ps(f"psT{i}", [128, D], BF16) for i in range(2)]
        psG = [ps(f"psG{i}", [128, 512], F32) for i in range(4)]
        psO = [ps(f"psO{i}", [128, 512], F32) for i in range(2)]

        sem_names = ["pe", "act", "dve", "pool", "ld0", "ld1", "ld2", "st0", "st1", "st2",
                     "su0", "su1", "su2", "su3", "su4", "su5", "su6"] + [f"w{l}{g}" for l in range(2) for g in range(4)]
        sems = {n: es.enter_context(nc.semaphore(n)) for n in sem_names}
        block = es.enter_context(nc.Block())

        S = Sched()

        gpreT = lambda l, dk: cst[:, l * 8 + dk:l * 8 + dk + 1]
        cwc = lambda j, k: cst[:, 16 + j * 3 + k:16 + j * 3 + k + 1]
        pscl = lambda j: cst[:, 28 + j:29 + j]
        lngc = lambda h: cst[:, 32 + h:33 + h]
        lnbc = lambda h: cst[:, 40 + h:41 + h]
        neghalf = cst[:, 112:113]
        neghalf2 = cst[:, 112:114]
        identf = tmp[NTMP - 1][:, 0:128]
        maskv = tmp[NTMP - 1][:, 128:256]
        onesf = tmp[NTMP - 2][:, 0:128]
        rcnt = lambda g: small[:, 128 + g * 16:128 + (g + 1) * 16]

        B_cst = Buf(); B_gpo = Buf(); B_cterm = Buf(); B_wsT = Buf(); B_plw = Buf()
        B_identb = Buf(); B_rcnt = Buf()
        B_x = [[Buf(), Buf()] for _ in range(3)]
        B_xsh = [[[Buf(), Buf()], [Buf(), Buf()]] for _ in range(2)]
        B_hT = [[Buf(), Buf()], [Buf(), Buf()]]
        B_yT = [[Buf() for _ in range(8)], [Buf() for _ in range(8)]]
        B_hT.append([B_yT[1][3], B_yT[1][3]])
        B_h = [Buf() for _ in range(4)]
        B_xp = [Buf() for _ in range(4)]
        B_tmp = [Buf() for _ in range(NTMP)]
        B_pooled = Buf()
        B_psT = [Buf(), Buf()]
        B_G = [Buf() for _ in range(4)]
        B_O = [Buf(), Buf()]
        B_small = {}

        small_init = [None]

        def smallbuf(name):
            if name not in B_small:
                B_small[name] = Buf()
                B_small[name].w = small_init[0]
            return B_small[name]

        tmp_rr = [0]

        def get_tmp():
            i = tmp_rr[0] % NTMP
            tmp_rr[0] += 1
            return tmp[i], B_tmp[i]


        wst_f = xres[2][:, 1, :].rearrange("p (h t) -> p h t", h=8)
        B_wstf = B_x[2][1]
        S.op("sp", ("dma_start", dict(out=cst[:, :], in_=cst_d[:, :])), writes=[B_cst], sem="su0", inc=16)
        B_c2 = B_tmp[NTMP - 1]
        B_c2o = B_tmp[NTMP - 2]
        S.op("sp", ("dma_start", dict(out=tmp[NTMP - 1][:, 0:256], in_=cst2_d[:, 0:256])), writes=[B_c2], sem="su1", inc=16)
        S.op("sp", ("dma_start", dict(out=tmp[NTMP - 2][:, 0:128], in_=cst2_d[:, 256:384])), writes=[B_c2o], sem="su6", inc=16)
        if 0 in layers:
            S.op("sp", ("dma_start", dict(out=xres[2][0:HAL, 0, :], in_=xh_d[:, :])),
                 writes=[B_x[2][0]], sem="ld2", inc=16)
        S.op("sp", ("dma_start", dict(out=xres[0][:, :, :],
                                      in_=x_d[0:T, :].rearrange("(tt p) d -> p tt d", p=128))),
             writes=[B_x[0][0], B_x[0][1]], sem="ld0", inc=16)
        if ntiles > 1:
            S.op("sp", ("dma_start", dict(out=xres[1][:, :, :],
                                          in_=x_d[T:2 * T, :].rearrange("(tt p) d -> p tt d", p=128))),
                 writes=[B_x[1][0], B_x[1][1]], sem="ld1", inc=16)
        S.op("pool", ("dma_start", dict(out=plw[:, :, :], in_=plw_d[:, :, :])), writes=[B_plw], sem="su4", inc=16)
        wtok = {l: [[], [], [], []] for l in range(2)}
        wready = {l: [None, None, None, None] for l in range(2)}

        wgroups = []
        cpos = {l: {n: p for p, n in enumerate(_chunk_order(l))} for l in range(2)}

        def wdma(l, grp, out_ap, in_ap):
            if not wgroups or wgroups[-1] is not wtok[l][grp]:
                wgroups.append(wtok[l][grp])
            gi = len(wgroups) - 1
            dep = list(wgroups[gi - 2]) if gi >= 2 else []
            t = S.op("pool", ("dma_start", dict(out=out_ap, in_=in_ap)), sem=f"w{l}{grp}", inc=16, extra=dep)
            wtok[l][grp].append(t)

        fold_rr = [0]

        def load_weights(l):
            g = OPT["wmerge"]
            for cb in range(3):
                for dk in range(0, 8, g):
                    wdma(l, cb, Win[l][:, dk:dk + g, cb * D:(cb + 1) * D],
                         win_d[l][dk * 128:(dk + g) * 128, cb * D:(cb + 1) * D].rearrange("(k p) n -> p k n", p=128))
                if OPT["wfold"]:
                    landed = list(wtok[l][cb])
                    toks = []
                    for dk in range(8):
                        eng = ("dve", "act", "pool", "dve", "act", "dve", "act", "pool")[fold_rr[0] % 8]
                        fold_rr[0] += 1
                        ap = Win[l][:, dk, cb * D:(cb + 1) * D]
                        if eng == "act":
                            call = ("activation", dict(out=ap, in_=ap, func=AF.Copy, scale=gpreT(l, dk)))
                        else:
                            call = ("tensor_scalar", dict(out=ap, in0=ap, scalar1=gpreT(l, dk), scalar2=0.0,
                                                          op0=ALU.mult, op1=ALU.add))
                        toks.append(S.op(eng, call, reads=[B_cst], extra=landed))
                    wready[l][cb] = toks
                else:
                    wready[l][cb] = wtok[l][cb]
            for ck in range(0, 8, g):
                wdma(l, 3, Wout[l][:, ck:ck + g, :],
                     wout_d[l][ck * 128:(ck + g) * 128, :].rearrange("(k p) n -> p k n", p=128))
            wready[l][3] = wtok[l][3]

        def emit_late_pieces(n):
            return

        late_pieces = []
        load_weights(layers[0])
        B_smallall = smallbuf("all")
        S.op("pool", ("memset", dict(ap=small[:, :], constant=0.0)), writes=[B_smallall])
        S.op("dve", ("reciprocal", dict(out=small[:, 128:192], in_=cst[:, 48:112])),
             reads=[B_cst], writes=[B_rcnt, B_smallall])
        small_init[0] = B_smallall.w
        for j in range(4):
            S.op("pool", ("memset", dict(ap=hbuf[j][:, :], constant=0.0)), writes=[B_h[j]])
            S.op("pool", ("memset", dict(ap=xpbuf[j][:, :], constant=0.0)), writes=[B_xp[j]])
        S.op("dve", ("tensor_copy", dict(out=identb[:, :], in_=identf)), reads=[B_c2], writes=[B_identb])
        if 1 in layers:
            S.op("sp", ("dma_start", dict(out=cterm[:, :, :], in_=bsb_d.rearrange("p (h t) -> p h t", h=8))),
                 writes=[B_cterm], sem="su2", inc=16)
            S.op("sp", ("dma_start", dict(out=wst_f, in_=wst_d[:, :, :])), writes=[B_wstf], sem="su3", inc=16)
        S.op("sp", ("dma_start", dict(out=gpo[:, :, :], in_=gpo_d.rearrange("p (l d) -> p l d", l=2))),
             writes=[B_gpo], sem="su5", inc=16)

        if 1 in layers:
            for h in range(8):
                S.op("dve", ("tensor_tensor", dict(out=wst_f[:, h, :], in0=wst_f[:, h, :], in1=maskv, op=ALU.mult)),
                     reads=[B_c2], writes=[B_wstf])
            S.op("dve", ("tensor_copy", dict(out=wsT[:, :, :], in_=wst_f)), reads=[B_wstf], writes=[B_wsT])
            for h in range(8):
                gi, hf = h // 4, h % 4
                S.op("pe", ("matmul", dict(out=psG[gi][:, hf * 128:(hf + 1) * 128], lhsT=onesf,
                                           rhs=wst_f[:, h, :], start=True, stop=True)),
                     reads=[B_c2o, B_wstf], writes=[B_G[gi]])
            for h in range(8):
                gi, hf = h // 4, h % 4
                S.op("dve", ("scalar_tensor_tensor", dict(
                    out=cterm[:, h, :], in0=psG[gi][:, hf * 128:(hf + 1) * 128], scalar=lnbc(h),
                    in1=cterm[:, h, :], op0=ALU.mult, op1=ALU.add)),
                    reads=[B_G[gi], B_cst], writes=[B_cterm])

        for l in layers[1:]:
            load_weights(l)

        ld_sem = ["ld0", "ld1", "ld2"]
        last_store = {}
        st_sem = ["st0", "st1", "st2"]

        def load_x(ti):
            b = ti % 3
            src = x_d[ti * T:(ti + 1) * T, :].rearrange("(tt p) d -> p tt d", p=128)
            S.op("sp", ("dma_start", dict(out=xres[b][:, :, :], in_=src)),
                 writes=[B_x[b][0], B_x[b][1]], sem=ld_sem[b], inc=16)

        def phase1(s, np_, c0, W, l, part_b_too=True, tts=None, xb=0, hs=None):
            nsub = (W + 127) // 128
            o = s * 56
            for tt in (range(nsub) if tts is None else tts):
                Bss, Bv1, Br = smallbuf(f"press{s}{tt}"), smallbuf(f"prev1{s}{tt}"), smallbuf(f"prer{s}{tt}")
                S.op("act", ("activation", dict(out=xs[s][0:np_, tt, :], in_=xres[xb][0:np_, tt, :],
                                                func=AF.Square, accum_out=small[0:np_, o + tt:o + tt + 1])),
                     reads=[B_x[xb][tt]], writes=[B_xsh[s][tt][0], B_xsh[s][tt][1], Bss])
                S.op("dve", ("tensor_scalar", dict(out=small[:, o + 2 + tt:o + 3 + tt], in0=small[:, o + tt:o + tt + 1],
                                                   scalar1=1.0 / D, scalar2=EPS, op0=ALU.mult, op1=ALU.add)),
                     reads=[Bss], writes=[Bv1])
                S.op("pool", ("tensor_tensor", dict(out=small[:, o + 4 + tt:o + 5 + tt], in0=small[:, o + 2 + tt:o + 3 + tt],
                                                    in1=neghalf, op=ALU.pow)),
                     reads=[Bv1, B_cst], writes=[Br])
                rs = small[0:np_, o + 4 + tt:o + 5 + tt]
                S.op("act", ("activation", dict(out=xs[s][0:np_, tt, 0:512], in_=xres[xb][0:np_, tt, 0:512], func=AF.Copy,
                                                scale=rs)),
                     reads=[B_x[xb][tt], Br], writes=[B_xsh[s][tt][0]])
                if OPT["xsplit"]:
                    S.op("dve", ("tensor_scalar", dict(out=xs[s][0:np_, tt, 512:1024], in0=xres[xb][0:np_, tt, 512:1024],
                                                       scalar1=rs, scalar2=None, op0=ALU.mult)),
                         reads=[B_x[xb][tt], Br], writes=[B_xsh[s][tt][1]])
                else:
                    S.op("act", ("activation", dict(out=xs[s][0:np_, tt, 512:1024], in_=xres[xb][0:np_, tt, 512:1024],
                                                    func=AF.Copy, scale=rs)),
                         reads=[B_x[xb][tt], Br], writes=[B_xsh[s][tt][1]])
            if not part_b_too:
                return
            phase1b(s, np_, c0, W, l, tts, hs)

        def phase1b(s, np_, c0, W, l, tts=None, hs=None):
            hs = s if hs is None else hs
            nsub = (W + 127) // 128
            for tt in (range(nsub) if tts is None else tts):
                for hf in range(2):
                    calls = [("transpose", dict(out=psT[tt][:, dk * 128:dk * 128 + np_],
                                                in_=xs[s][0:np_, tt, dk * 128:(dk + 1) * 128],
                                                identity=identb[0:np_, 0:np_])) for dk in range(4 * hf, 4 * hf + 4)]
                    S.op("pe", calls, reads=[B_xsh[s][tt][hf], B_identb], writes=[B_psT[tt]])
                cs0 = c0 + tt * 128 - hT_off[hs]
                if OPT["wfold"]:
                    S.op("dve", ("tensor_copy", dict(
                        out=hT[hs][:, :, cs0:cs0 + np_],
                        in_=psT[tt][:, :].rearrange("p (k t) -> p k t", k=8)[:, :, 0:np_])),
                        reads=[B_psT[tt]], writes=[B_hT[hs][tt]])
                else:
                    S.op("dve", ("tensor_tensor", dict(
                        out=hT[hs][:, :, cs0:cs0 + np_],
                        in0=psT[tt][:, :].rearrange("p (k t) -> p k t", k=8)[:, :, 0:np_],
                        in1=cst[:, l * 8:(l + 1) * 8].unsqueeze(2).to_broadcast([128, 8, np_]), op=ALU.mult)),
                        reads=[B_psT[tt], B_cst], writes=[B_hT[hs][tt]])

        def inproj_chunk(l, s, n, dst_ap, dstB, c0, W):
            p = cpos[l][n]
            calls = [("matmul", dict(out=dst_ap, lhsT=Win[l][:, dk, p * 128:(p + 1) * 128],
                                     rhs=hT[s][:, dk, c0 - hT_off[s]:c0 - hT_off[s] + W], start=(dk == 0), stop=(dk == 7)))
                     for dk in range(8)]
            S.op("pe", calls, reads=[B_hT[s][0], B_hT[s][1]], writes=[dstB], extra=wready[l][p // 8])

        gcount = [0]
        pending = []

        def even_group(l, s, ti, c0, W, kind, j):
            cs = slice(c0, c0 + W)
            hs = slice(HAL + c0, HAL + c0 + W)
            p = gcount[0] % 2
            gcount[0] += 1
            g0, g1 = psG[2 * p], psG[2 * p + 1]
            Bg0, Bg1 = B_G[2 * p], B_G[2 * p + 1]
            if kind == "A":
                xa_ps, gc_ps = g0[:, 0:W], g0[:, 256:256 + W]
                gb_ps, za_ps = g1[:, 0:W], g1[:, 256:256 + W]
                inproj_chunk(l, s, j, xa_ps, Bg0, c0, W)
                inproj_chunk(l, s, 8 + j, gc_ps, Bg0, c0, W)
                if ti >= 0:
                    inproj_chunk(l, s, 4 + j, gb_ps, Bg1, c0, W)
                    inproj_chunk(l, s, 12 + j, za_ps, Bg1, c0, W)
                for f in pending:
                    f()
                del pending[:]
                tA, BA = get_tmp()
                tS, BS = get_tmp()
                t1, B1 = get_tmp()
                S.op("act", ("activation", dict(out=tA[:, 0:W], in_=xa_ps, func=AF.Copy)), reads=[Bg0], writes=[BA])
                if ti >= 0:
                    S.op("act", ("activation", dict(out=tS[:, 0:W], in_=za_ps, func=AF.Silu)), reads=[Bg1], writes=[BS])
                if c0 == 0:
                    S.op("dve", ("tensor_copy", dict(out=hbuf[j][:, 0:HAL], in_=hbuf[j][:, T:T + HAL])),
                         reads=[B_h[j]], writes=[B_h[j]])
                S.op("dve", ("tensor_tensor", dict(out=hbuf[j][:, hs], in0=gc_ps, in1=tA[:, 0:W], op=ALU.mult)),
                     reads=[Bg0, BA], writes=[B_h[j]])
                if ti < 0:
                    return
                S.op("dve", ("tensor_tensor", dict(out=tS[:, 0:W], in0=gb_ps, in1=tS[:, 0:W], op=ALU.mult)),
                     reads=[Bg1], writes=[BS])
                S.op("act", ("activation", dict(out=t1[:, 0:W], in_=hbuf[j][:, HAL + c0 - 2:HAL + c0 - 2 + W],
                                                func=AF.Copy, scale=cwc(j, 0))),
                     reads=[B_h[j], B_cst], writes=[B1])
                S.op("dve", ("scalar_tensor_tensor", dict(
                    out=t1[:, 0:W], in0=hbuf[j][:, HAL + c0 - 1:HAL + c0 - 1 + W], scalar=cwc(j, 1),
                    in1=t1[:, 0:W], op0=ALU.mult, op1=ALU.add)), reads=[B_h[j], B_cst], writes=[B1])
                S.op("dve", ("scalar_tensor_tensor", dict(
                    out=t1[:, 0:W], in0=hbuf[j][:, hs], scalar=cwc(j, 2),
                    in1=t1[:, 0:W], op0=ALU.mult, op1=ALU.add)), reads=[B_h[j], B_cst], writes=[B1])
                S.op("pool", ("tensor_tensor", dict(out=yT[s][:, j, cs], in0=t1[:, 0:W], in1=tS[:, 0:W], op=ALU.mult)),
                     reads=[B1, BS], writes=[B_yT[s][j]])
            else:
                w = 2 << j
                xp_ps, zp_ps = g0[:, 0:W], g1[:, 0:W]
                inproj_chunk(l, s, 16 + j, xp_ps, Bg0, c0, W)
                if ti >= 0:
                    inproj_chunk(l, s, 20 + j, zp_ps, Bg1, c0, W)
                for f in pending:
                    f()
                del pending[:]
                tZ, BZ = get_tmp()
                if c0 == 0:
                    S.op("dve", ("tensor_copy", dict(out=xpbuf[j][:, 0:HAL], in_=xpbuf[j][:, T:T + HAL])),
                         reads=[B_xp[j]], writes=[B_xp[j]])
                S.op("act", ("activation", dict(out=xpbuf[j][:, hs], in_=xp_ps, func=AF.Copy)),
                     reads=[Bg0], writes=[B_xp[j]])
                if ti < 0:
                    return
                S.op("act", ("activation", dict(out=tZ[:, 0:W], in_=zp_ps, func=AF.Silu)), reads=[Bg1], writes=[BZ])
                src, Bsrc = xpbuf[j], B_xp[j]
                ws = 1
                while ws < w:
                    lo = HAL + c0 - (w - 2 * ws)
                    hi = HAL + c0 + W
                    dst, Bdst = get_tmp()
                    add_eng = "pool" if (OPT["adds"] == "pool" or (OPT["adds"] == "mixed" and 2 * ws < w)) else "dve"
                    S.op(add_eng, ("tensor_tensor", dict(out=dst[:, lo:hi], in0=src[:, lo:hi],
                                                        in1=src[:, lo - ws:hi - ws], op=ALU.add)),
                         reads=[Bsrc], writes=[Bdst])
                    src, Bsrc = dst, Bdst
                    ws *= 2
                S.op("dve", ("scalar_tensor_tensor", dict(
                    out=pooled[:, cs], in0=src[:, hs], scalar=1.0 / w, in1=xpbuf[j][:, hs],
                    op0=ALU.mult, op1=ALU.subtract)), reads=[Bsrc, B_xp[j]], writes=[B_pooled])
                if ti == 0 and c0 == 0:
                    t16, B16 = get_tmp()
                    S.op("dve", ("tensor_tensor", dict(out=t16[:, 0:HAL], in0=src[:, HAL:2 * HAL], in1=rcnt(j),
                                                       op=ALU.mult)), reads=[Bsrc, B_rcnt], writes=[B16])
                    S.op("dve", ("tensor_tensor", dict(out=pooled[:, 0:HAL], in0=t16[:, 0:HAL],
                                                       in1=xpbuf[j][:, HAL:2 * HAL], op=ALU.subtract)),
                         reads=[B16, B_xp[j]], writes=[B_pooled])
                mx_ps = g1[:, 256:256 + W]

                def do_pool():
                    S.op("pe", ("matmul", dict(out=mx_ps, lhsT=plw[:, j, :], rhs=pooled[:, cs], start=True, stop=True)),
                         reads=[B_pooled, B_plw], writes=[Bg1])
                    S.op("dve", ("scalar_tensor_tensor", dict(
                        out=yT[s][:, 4 + j, cs], in0=mx_ps, scalar=pscl(j), in1=tZ[:, 0:W],
                        op0=ALU.mult, op1=ALU.mult)), reads=[Bg1, BZ, B_cst], writes=[B_yT[s][4 + j]])
                pending.append(do_pool)

        def even_phase2(l, s, ti, c0, W, late=0, hooks=None):
            gcount[0] = 1
            g = 0
            for j in range(4):
                for kind in ("A", "B"):
                    even_group(l, s, ti, c0, W, kind, j)
                    emit_late_pieces(late)
                    if hooks and g in hooks:
                        hooks[g]()
                    g += 1

        def odd_head(l, s, c0, W, h):
            cs = slice(c0, c0 + W)
            p = gcount[0] % 2
            gcount[0] += 1
            g0, g1 = psG[2 * p], psG[2 * p + 1]
            Bg0, Bg1 = B_G[2 * p], B_G[2 * p + 1]
            u_ps, z_ps = g0[:, 0:W], g1[:, 0:W]
            inproj_chunk(l, s, h, u_ps, Bg0, c0, W)
            inproj_chunk(l, s, 16 + h, z_ps, Bg1, c0, W)
            for f in pending:
                f()
            del pending[:]
            calls = [("matmul", dict(out=g0[:, 256 + tt * 128:256 + (tt + 1) * 128],
                                     lhsT=xs[s][:, tt, h * 128:(h + 1) * 128], rhs=wsT[:, h, :],
                                     start=True, stop=True)) for tt in range(2)]
            S.op("pe", calls, reads=[B_xsh[s][0][h // 4], B_xsh[s][1][h // 4], B_wsT], writes=[Bg0])
            tZ, BZ = get_tmp()
            tV, BV = get_tmp()
            S.op("act", ("activation", dict(out=tZ[:, 0:W], in_=z_ps, func=AF.Silu)), reads=[Bg1], writes=[BZ])
            for tt in range(2):
                S.op("dve", ("scalar_tensor_tensor", dict(
                    out=tV[:, tt * 128:(tt + 1) * 128], in0=g0[:, 256 + tt * 128:256 + (tt + 1) * 128],
                    scalar=lngc(h), in1=cterm[:, h, :], op0=ALU.mult, op1=ALU.add)),
                    reads=[Bg0, B_cterm, B_cst], writes=[BV])
            S.op("dve", ("tensor_tensor", dict(out=tZ[:, 0:W], in0=u_ps, in1=tZ[:, 0:W], op=ALU.mult)),
                 reads=[Bg0], writes=[BZ])
            S.op("pool", ("tensor_tensor", dict(out=yT[s][:, h, cs], in0=tV[:, 0:W], in1=tZ[:, 0:W], op=ALU.mult)),
                 reads=[BV, BZ], writes=[B_yT[s][h]])

        def odd_phase2(l, s, ti, c0, W, hooks=None):
            o = s * 56
            for tt in range(2):
                banks = psO if tt == 0 else psG[0:2]
                Bb = B_O if tt == 0 else B_G[0:2]
                for eh in range(2):
                    if OPT["nsplit"]:
                        calls = [("matmul", dict(out=banks[eh][:, hh * 256:(hh + 1) * 256],
                                                 lhsT=hT[s][:, dk, tt * 128:(tt + 1) * 128],
                                                 rhs=Win[l][:, dk, eh * 512 + hh * 256:eh * 512 + (hh + 1) * 256],
                                                 start=(dk == 0), stop=(dk == 7))) for hh in range(2) for dk in range(8)]
                    else:
                        calls = [("matmul", dict(out=banks[eh][:, :], lhsT=hT[s][:, dk, tt * 128:(tt + 1) * 128],
                                                 rhs=Win[l][:, dk, eh * 512:(eh + 1) * 512],
                                                 start=(dk == 0), stop=(dk == 7))) for dk in range(8)]
                    S.op("pe", calls, reads=[B_hT[s][tt]], writes=[Bb[eh]], extra=wready[l][0])
                Bst, Bmv = smallbuf(f"lnst{s}{tt}"), smallbuf(f"lnmv{s}{tt}")
                q = o + 16 + tt * 20
                st_ap = small[:, q:q + 12]
                mv_ap = small[:, q + 12:q + 14]
                v1_ap, r_ap, nm_ap = small[:, q + 14:q + 15], small[:, q + 15:q + 16], small[:, q + 16:q + 17]
                for eh in range(2):
                    S.op("dve", ("bn_stats", dict(out=st_ap[:, eh * 6:(eh + 1) * 6], in_=banks[eh][:, :])),
                         reads=[Bb[eh]], writes=[Bst])
                S.op("dve", ("bn_aggr", dict(out=mv_ap, in_=st_ap)), reads=[Bst], writes=[Bmv])
                Bv1, Br, Bnm = smallbuf(f"lnv1{s}{tt}"), smallbuf(f"lnr{s}{tt}"), smallbuf(f"lnnm{s}{tt}")
                S.op("dve", ("tensor_scalar", dict(out=v1_ap, in0=mv_ap[:, 1:2], scalar1=1.0, scalar2=EPS,
                                                   op0=ALU.mult, op1=ALU.add)), reads=[Bmv], writes=[Bv1])
                S.op("pool", ("tensor_tensor", dict(out=r_ap, in0=v1_ap, in1=neghalf, op=ALU.pow)),
                     reads=[Bv1, B_cst], writes=[Br])
                S.op("dve", ("scalar_tensor_tensor", dict(out=nm_ap, in0=mv_ap[:, 0:1], scalar=-1.0, in1=r_ap,
                                                          op0=ALU.mult, op1=ALU.mult)),
                     reads=[Bmv, Br], writes=[Bnm])
                for eh in range(2):
                    S.op("act", ("activation", dict(out=xs[s][:, tt, eh * 512:(eh + 1) * 512], in_=banks[eh][:, :],
                                                    func=AF.Identity, bias=nm_ap, scale=r_ap)),
                         reads=[Bb[eh], Br, Bnm], writes=[B_xsh[s][tt][eh]])
            gcount[0] = 1
            for h in range(8):
                odd_head(l, s, c0, W, h)
                if hooks and h in hooks:
                    hooks[h]()

        def phase3(l, s, ti, tt, last, alt=False):
            o = s * 56
            xb = ti % 3
            banks, Bb = (psG[0:2], B_G[0:2]) if alt else (psO, B_O)
            for eh in range(2):
                if OPT["nsplit"]:
                    calls = [("matmul", dict(out=banks[eh][:, hh * 256:(hh + 1) * 256],
                                             lhsT=yT[s][:, ck, tt * 128:(tt + 1) * 128],
                                             rhs=Wout[l][:, ck, eh * 512 + hh * 256:eh * 512 + (hh + 1) * 256],
                                             start=(ck == 0), stop=(ck == 7))) for hh in range(2) for ck in range(8)]
                else:
                    calls = [("matmul", dict(out=banks[eh][:, :], lhsT=yT[s][:, ck, tt * 128:(tt + 1) * 128],
                                             rhs=Wout[l][:, ck, eh * 512:(eh + 1) * 512],
                                             start=(ck == 0), stop=(ck == 7))) for ck in range(8)]
                S.op("pe", calls, reads=B_yT[s], writes=[Bb[eh]], extra=wready[l][3])
            q = o + 8 + tt * 4
            Bss = smallbuf(f"postss{s}{tt}")
            for eh in range(2):
                tj, bj = get_tmp()
                S.op("act", ("activation", dict(out=tj[:, 0:256].bitcast(BF16), in_=banks[eh][:, :], func=AF.Square,
                                                accum_out=small[:, q + eh:q + eh + 1])),
                     reads=[Bb[eh]], writes=[bj, Bss])
            Bs1, Br = smallbuf(f"posts1{s}{tt}"), smallbuf(f"postr{s}{tt}")
            S.op("dve", ("tensor_tensor", dict(out=small[:, q + 2:q + 3], in0=small[:, q:q + 1],
                                               in1=small[:, q + 1:q + 2], op=ALU.add)), reads=[Bss], writes=[Bs1])
            S.op("dve", ("tensor_scalar", dict(out=small[:, q + 2:q + 3], in0=small[:, q + 2:q + 3], scalar1=1.0 / D,
                                               scalar2=EPS, op0=ALU.mult, op1=ALU.add)), reads=[Bs1], writes=[Bs1])
            S.op("pool", ("tensor_tensor", dict(out=small[:, q + 3:q + 4], in0=small[:, q + 2:q + 3], in1=neghalf,
                                                op=ALU.pow)), reads=[Bs1, B_cst], writes=[Br])
            for eh in range(2):
                for qq in range(2):
                    tm, bm = get_tmp()
                    e0 = eh * 512 + qq * 256
                    S.op("dve", ("scalar_tensor_tensor", dict(
                        out=tm[:, 0:256], in0=banks[eh][:, qq * 256:(qq + 1) * 256], scalar=small[:, q + 3:q + 4],
                        in1=gpo[:, l, e0:e0 + 256], op0=ALU.mult, op1=ALU.mult)),
                        reads=[Bb[eh], Br, B_gpo], writes=[bm])
                    S.op("pool", ("tensor_tensor", dict(out=xres[xb][:, tt, e0:e0 + 256],
                                                        in0=xres[xb][:, tt, e0:e0 + 256], in1=tm[:, 0:256],
                                                        op=ALU.add)), reads=[bm], writes=[B_x[xb][tt]])
            if last and tt == 1:
                dst = out_d[ti * T:(ti + 1) * T, :].rearrange("(tt p) d -> p tt d", p=128)
                last_store[xb] = S.op("sp", ("dma_start", dict(out=dst, in_=xres[xb][:, :, :])),
                                     reads=[B_x[xb][0], B_x[xb][1]], sem=st_sem[xb], inc=16)

        nl = len(layers)
        loaded = {0, 1}

        def ensure_loaded(ti):
            if ti < ntiles and ti not in loaded:
                loaded.add(ti)
                load_x(ti)

        handed = [False]

        def handoff(s):
            return

        def do_p1(blk, tts=None):
            ti, s, l, li = blk
            ensure_loaded(ti)
            phase1(s, 128, 0, T, l, True, tts, ti % 3)

        def do_p3(blk, tt, alt=False):
            ti, s, l, li = blk
            phase3(l, s, ti, tt, li == nl - 1, alt)
            if li == nl - 1 and tt == 1:
                ensure_loaded(ti + 3)

        groups = []
        t0_ = 0
        sizes = []
        rem = ntiles
        while rem > 0:
            g = 3 if (rem >= 3 and rem != 4) else 2 if rem >= 2 else 1
            sizes.append(g)
            rem -= g
        for g in sizes:
            groups.append(list(range(t0_, t0_ + g)))
            t0_ += g
        blocks = []
        for grp in groups:
            for li, l in enumerate(layers):
                for ti in grp:
                    blocks.append((ti, len(blocks) % 2, l, li))
        super_ok = len(blocks) >= 2 and blocks[0][2] == 0 and blocks[1][2] == 0 and blocks[1][1] == 1
        if 0 in layers:
            phase1(0, HAL, T - HAL, HAL, 0, True, None, 2, 2)
            if not super_ok:
                even_phase2(0, 2, -1, T - HAL, HAL)
                for f in pending:
                    f()
                del pending[:]
        ensure_loaded(2)
        do_p1(blocks[0])
        k_start = 0
        if super_ok:
            b0, b1 = blocks[0], blocks[1]
            do_p1(b1)
            gcount[0] = 1
            for kind, j in (("A", 0), ("A", 1), ("B", 0), ("B", 1), ("B", 2), ("B", 3), ("A", 2), ("A", 3)):
                even_group(0, 2, -1, T - HAL, HAL, kind, j)
                even_group(0, b0[1], b0[0], 0, T, kind, j)
                even_group(0, b1[1], b1[0], 0, T, kind, j)
            for f in pending:
                f()
            del pending[:]
            if len(blocks) > 2:
                nx = blocks[2]
                chained0 = nx[0] == b0[0]
                if not chained0:
                    do_p1(nx)
                do_p3(b0, 0)
                if chained0:
                    do_p1(nx, [0])
                do_p3(b0, 1)
                if chained0:
                    do_p1(nx, [1])
            else:
                do_p3(b0, 0)
                do_p3(b0, 1)
                do_p3(b1, 0)
                do_p3(b1, 1)
            k_start = 2
        for k, blk in enumerate(blocks):
            if k < k_start:
                continue
            ti, s, l, li = blk
            prev = blocks[k - 1] if k > 0 else None
            nxt = blocks[k + 1] if k + 1 < len(blocks) else None
            emb3 = prev is not None and prev[0] != ti
            emb1 = nxt is not None and nxt[0] != ti
            chained = emb1 and emb3 and nxt[0] == prev[0]

            def h0(prev=prev, nxt=nxt, emb3=emb3, emb1=emb1, chained=chained):
                if emb1 and not chained:
                    do_p1(nxt)
                if emb3:
                    do_p3(prev, 0)
                    if chained:
                        do_p1(nxt, [0])

            def h2(prev=prev, nxt=nxt, emb3=emb3, emb1=emb1, chained=chained):
                if emb3:
                    do_p3(prev, 1)
                    if chained:
                        do_p1(nxt, [1])

            hooks = {0: h0, 2: h2}
            if l == 0:
                even_phase2(l, s, ti, 0, T, 0, hooks)
            else:
                odd_phase2(l, s, ti, 0, T, hooks)
            if nxt is None or nxt[0] == ti:
                for f in pending:
                    f()
                del pending[:]
                do_p3(blk, 0)
                do_p3(blk, 1, True)
                if nxt is not None:
                    do_p1(nxt)

        S.op("sp", ("nop", dict()), extra=list(last_store.values()), mark=False)
        S.schedule()

        @block.tensor
        def _(e):
            S.emit("pe", e, sems)

        @block.scalar
        def _(e):
            S.emit("act", e, sems)

        @block.vector
        def _(e):
            S.emit("dve", e, sems)

        @block.gpsimd
        def _(e):
            S.emit("pool", e, sems)

        @block.sync
        def _(e):
            S.emit("sp", e, sems)

    return nc


def _host_consts(core, pre_norm, post_norm, conv_w, pool_scale, ln_g, ln_b):
    c = np.zeros((128, NA), np.float32)
    for l in range(2):
        c[:, l * 8:(l + 1) * 8] = pre_norm[l].reshape(8, 128).T
    c[:, 16:28] = conv_w.reshape(3, 4, 128).transpose(2, 1, 0).reshape(128, 12)
    c[:, 28:32] = pool_scale.reshape(4, 128).T
    c[:, 32:40] = ln_g.reshape(8, 128).T
    c[:, 40:48] = ln_b.reshape(8, 128).T
    pos = np.arange(16)
    for g, w in enumerate((2, 4, 8, 16)):
        cnt = np.minimum(pos + 1, w) if core % 2 == 0 else np.full(16, w)
        c[:, 48 + g * 16:48 + (g + 1) * 16] = cnt.astype(np.float32)[None, :]
    c[:, 112:114] = -0.5
    return c


def _host_consts2():
    c = np.zeros((128, NA2), np.float32)
    c[:, 0:128] = np.eye(128, dtype=np.float32)
    s = np.arange(128)
    c[:, 128:256] = (s[:, None] <= s[None, :]).astype(np.float32)
    c[:, 256:384] = 1.0
    return c


_NC_CACHE = {}


def _run(layers, xf, pre_norm, post_norm, even_w_in, even_conv_w, even_pool_w, even_pool_scale,
         even_w_out, odd_w_in, odd_ln_g, odd_ln_b, odd_w_s, odd_b_s, odd_w_out):
    key = tuple(layers)
    if key not in _NC_CACHE:
        _NC_CACHE[key] = build_nc(layers)
    nc = _NC_CACHE[key]
    TOK = NT * T
    gpo = np.ascontiguousarray(np.broadcast_to(post_norm.reshape(1, 2 * D), (128, 2 * D)))
    bsb = np.ascontiguousarray(np.broadcast_to(odd_b_s[0].reshape(1, D), (128, D)))
    wst = np.ascontiguousarray(odd_w_s[0].transpose(2, 0, 1))
    plw = np.ascontiguousarray(even_pool_w[0].transpose(1, 0, 2))
    win0p = _permute_w_in(even_w_in[0], 0)
    win1p = _permute_w_in(odd_w_in[0], 1)
    in_maps = []
    for c in range(NCORES):
        lo = c * TOK
        if c % 2 == 0:
            xh = np.zeros((HAL, D), np.float32)
        else:
            xh = np.ascontiguousarray(xf[lo - HAL:lo])
        in_maps.append({
            "x": np.ascontiguousarray(xf[lo:lo + TOK]),
            "xh": xh,
            "cst": _host_consts(c, pre_norm, post_norm, even_conv_w[0], even_pool_scale[0],
                                odd_ln_g[0], odd_ln_b[0]),
            "cst2": _host_consts2(),
            "gpo": gpo, "bsb": bsb, "wst": wst, "plw": plw,
            "win0": win0p, "win1": win1p,
            "wout0": even_w_out[0], "wout1": odd_w_out[0],
        })
    res = run_bass_kernel_spmd(nc, in_maps, core_ids=list(range(NCORES)))
    return np.concatenate([np.asarray(r["out"]) for r in res.results], axis=0)


def kernel(**inputs):
    a = {k: np.ascontiguousarray(np.asarray(v, dtype=np.float32)) for k, v in inputs.items()}
    xf = a["x"].reshape(-1, D)
    out = _run(LAYERS, xf, a["pre_norm"], a["post_norm"], a["even_w_in"], a["even_conv_w"],
               a["even_pool_w"], a["even_pool_scale"], a["even_w_out"], a["odd_w_in"],
               a["odd_ln_g"], a["odd_ln_b"], a["odd_w_s"], a["odd_b_s"], a["odd_w_out"])
    return out.reshape(a["x"].shape).astype(np.float32)
```
